# Optimizing a Trainium2 kernel written in Bass

```python
import math
import jax, jax.numpy as jnp
from jax import lax
import numpy as np


D_MODEL = 1024
BATCH = 8
SEQ = 4096
DEPTH = 1

CTX_LEN = 256
GRID_W = 64
RET_WIDTH = 512
LRU_WIDTH = 512
MIX_WIDTH = RET_WIDTH + LRU_WIDTH
RET_HEADS = 4
RET_HEAD_DIM = RET_WIDTH // RET_HEADS
RET_CHUNK = 128
LRU_BLOCKS = 8
LRU_BLOCK_DIM = LRU_WIDTH // LRU_BLOCKS
LRU_C = 8.0
CONV_WIDTH = 4
MLP_HIDDEN = 4 * D_MODEL
ROPE_BASE = 10000.0
NORM_EPS = 1e-6
N_MOD = 6
IN_COLS = 4 * RET_WIDTH + 2 * LRU_WIDTH
IN_SPLITS = (RET_WIDTH, 2 * RET_WIDTH, 3 * RET_WIDTH, 4 * RET_WIDTH, 4 * RET_WIDTH + LRU_WIDTH)

kernel_name = "hybrid_retention_rglru_dit_block"


def rmsnorm(x, g):
    xf = x.astype(jnp.float32)
    y = xf * lax.rsqrt(jnp.mean(xf * xf, axis=-1, keepdims=True) + NORM_EPS)
    return (y * g.astype(jnp.float32)).astype(x.dtype)


def modulate(h, shift, scale):
    return h * (1.0 + scale) + shift


def to_heads(a):
    b, t, _ = a.shape
    return a.reshape(b, t, RET_HEADS, RET_HEAD_DIM).astype(jnp.float32)


def head_groupnorm(y):
    mu = jnp.mean(y, axis=-1, keepdims=True)
    yc = y - mu
    var = jnp.mean(yc * yc, axis=-1, keepdims=True)
    return yc * lax.rsqrt(var + NORM_EPS)


def axial_rotary_tables(t_len):
    rows = t_len // GRID_W
    row = jnp.repeat(jnp.arange(rows, dtype=jnp.float32), GRID_W)
    col = jnp.tile(jnp.arange(GRID_W, dtype=jnp.float32), rows)
    n_freq = RET_HEAD_DIM // 4
    inv = ROPE_BASE ** (-jnp.arange(n_freq, dtype=jnp.float32) / n_freq)
    ang = jnp.concatenate([row[:, None] * inv, col[:, None] * inv], axis=-1)
    return jnp.cos(ang), jnp.sin(ang)


def apply_rotary(x, cos, sin):
    half = RET_HEAD_DIM // 2
    x1, x2 = x[..., :half], x[..., half:]
    c = cos[None, :, None, :]
    s = sin[None, :, None, :]
    return jnp.concatenate([x1 * c - x2 * s, x2 * c + x1 * s], axis=-1)


def retention_scan(q, k, v, log_g, s0):
    b, t, h, dh = q.shape
    n = t // RET_CHUNK

    def chunks(a):
        return a.reshape(b, n, RET_CHUNK, h, dh).transpose(0, 3, 1, 2, 4)

    qc, kc, vc = chunks(q), chunks(k), chunks(v)
    pos = jnp.arange(RET_CHUNK, dtype=jnp.float32)
    lg = log_g[:, None]
    rel = pos[:, None] - pos[None, :]
    decay = jnp.where(rel >= 0, jnp.exp(lg[:, :, None] * jnp.maximum(rel, 0.0)), 0.0)
    scores = jnp.einsum('bhncd,bhnmd->bhncm', qc, kc) * decay[None, :, None]
    intra = jnp.einsum('bhncm,bhnme->bhnce', scores, vc)
    w_state = jnp.exp(lg * (RET_CHUNK - 1.0 - pos))
    u = jnp.einsum('bhncd,bhnce->bhnde', kc * w_state[None, :, None, :, None], vc)
    g_chunk = jnp.exp(lg * RET_CHUNK)[None, :, :, None]

    def step(s, u_n):
        return g_chunk * s + u_n, s

    _, s_prev = lax.scan(step, s0, u.transpose(2, 0, 1, 3, 4))
    s_prev = s_prev.transpose(1, 2, 0, 3, 4)
    w_query = jnp.exp(lg * (pos + 1.0))
    cross = jnp.einsum('bhncd,bhnde->bhnce', qc * w_query[None, :, None, :, None], s_prev)
    return (intra + cross).transpose(0, 2, 3, 1, 4).reshape(b, t, h, dh)


def retention_bidir(q, k, v, log_g_f, log_g_b, s0_f, s0_b):
    o_f = retention_scan(q, k, v, log_g_f, s0_f)
    o_b = retention_scan(q[:, ::-1], k[:, ::-1], v[:, ::-1], log_g_b, s0_b)[:, ::-1]
    return o_f + o_b


def retention_final_state(k, v, log_g, reverse):
    l_len = k.shape[1]
    pos = jnp.arange(l_len, dtype=jnp.float32)
    steps = pos if reverse else (l_len - 1.0 - pos)
    w = jnp.exp(log_g[:, None] * steps[None, :])
    return jnp.einsum('blhd,blhe->bhde', k * w.T[None, :, :, None], v)


def centred_depthwise_conv(x, w, bias):
    t_len = x.shape[1]
    left = (CONV_WIDTH - 1) // 2
    right = CONV_WIDTH - 1 - left
    xp = jnp.pad(x, ((0, 0), (left, right), (0, 0)))
    out = bias + xp[:, 0:t_len] * w[0]
    for j in range(1, CONV_WIDTH):
        out = out + xp[:, j:j + t_len] * w[j]
    return out


def linear_scan(a, b, h0, reverse):
    if reverse:
        a, b = a[:, ::-1], b[:, ::-1]
    b = b.at[:, 0].add(a[:, 0] * h0)

    def combine(lhs, rhs):
        return lhs[0] * rhs[0], rhs[0] * lhs[1] + rhs[1]

    _, h = lax.associative_scan(combine, (a, b), axis=1)
    final = h[:, -1]
    if reverse:
        h = h[:, ::-1]
    return h, final


def rglru_direction(xc, w_a, b_a, w_x, b_x, lam, h0, reverse):
    b, t, _ = xc.shape
    xf = xc.astype(jnp.float32)
    xb = xf.reshape(b, t, LRU_BLOCKS, LRU_BLOCK_DIM)
    r = jax.nn.sigmoid(jnp.einsum('btnd,nde->btne', xb, w_a.astype(jnp.float32)).reshape(b, t, LRU_WIDTH) + b_a.astype(jnp.float32))
    i = jax.nn.sigmoid(jnp.einsum('btnd,nde->btne', xb, w_x.astype(jnp.float32)).reshape(b, t, LRU_WIDTH) + b_x.astype(jnp.float32))
    log_a = -LRU_C * r * jax.nn.softplus(-lam.astype(jnp.float32))
    a = jnp.exp(log_a)
    inp = jnp.sqrt(-jnp.expm1(2.0 * log_a)) * (i * xf)
    return linear_scan(a, inp, h0, reverse)


def mix_output(o_ret, g, h_lru, gate, w_out):
    b, t = g.shape[0], g.shape[1]
    ret = jax.nn.silu(g.astype(jnp.float32)) * head_groupnorm(o_ret).reshape(b, t, RET_WIDTH)
    lru = h_lru * jax.nn.gelu(gate.astype(jnp.float32))
    return jnp.concatenate([ret, lru], axis=-1).astype(w_out.dtype) @ w_out


def squared_relu_mlp(h, w1, w2):
    return jnp.square(jax.nn.relu(h @ w1)) @ w2


def setup_inputs(seed: int = 0) -> dict:
    key = jax.random.key(seed)
    ks = jax.random.split(key, 24)
    f32 = jnp.float32

    def nrm(k, shape, scale):
        return jax.random.normal(k, shape, f32) * scale

    x = nrm(ks[0], (BATCH, SEQ, D_MODEL), 1.0)
    c = nrm(ks[1], (BATCH, D_MODEL), 1.0)
    ctx = nrm(ks[2], (BATCH, CTX_LEN, D_MODEL), 1.0)
    c_ctx = nrm(ks[3], (D_MODEL,), 1.0)
    w_ada = nrm(ks[4], (DEPTH, D_MODEL, N_MOD * D_MODEL), 0.5 * D_MODEL ** -0.5)
    b_ada = nrm(ks[5], (DEPTH, N_MOD * D_MODEL), 0.01)
    norm1_g = 1.0 + nrm(ks[6], (DEPTH, D_MODEL), 0.02)
    norm2_g = 1.0 + nrm(ks[7], (DEPTH, D_MODEL), 0.02)
    w_in = nrm(ks[8], (DEPTH, D_MODEL, IN_COLS), D_MODEL ** -0.5)
    gamma = 1.0 - 2.0 ** (-5.0 - jnp.arange(RET_HEADS, dtype=f32))
    ret_decay = jnp.log(gamma) - jnp.log1p(-gamma) + nrm(ks[9], (DEPTH, 2, RET_HEADS), 0.05)
    conv_w = nrm(ks[10], (DEPTH, CONV_WIDTH, LRU_WIDTH), CONV_WIDTH ** -0.5)
    conv_b = nrm(ks[11], (DEPTH, LRU_WIDTH), 0.01)
    lru_wa = nrm(ks[12], (DEPTH, 2, LRU_BLOCKS, LRU_BLOCK_DIM, LRU_BLOCK_DIM), LRU_BLOCK_DIM ** -0.5)
    lru_ba = nrm(ks[13], (DEPTH, 2, LRU_WIDTH), 0.01)
    lru_wx = nrm(ks[14], (DEPTH, 2, LRU_BLOCKS, LRU_BLOCK_DIM, LRU_BLOCK_DIM), LRU_BLOCK_DIM ** -0.5)
    lru_bx = nrm(ks[15], (DEPTH, 2, LRU_WIDTH), 0.01)
    u = jax.random.uniform(ks[16], (DEPTH, 2, LRU_WIDTH), f32, 0.9, 0.999)
    a0 = u ** (1.0 / LRU_C)
    lru_lambda = jnp.log(a0) - jnp.log1p(-a0)
    w_out = nrm(ks[17], (DEPTH, MIX_WIDTH, D_MODEL), MIX_WIDTH ** -0.5)
    w_mlp1 = nrm(ks[18], (DEPTH, D_MODEL, MLP_HIDDEN), D_MODEL ** -0.5)
    w_mlp2 = nrm(ks[19], (DEPTH, MLP_HIDDEN, D_MODEL), MLP_HIDDEN ** -0.5)
    final_g = 1.0 + nrm(ks[20], (D_MODEL,), 0.02)
    return dict(x=x, c=c, ctx=ctx, c_ctx=c_ctx, w_ada=w_ada, b_ada=b_ada, norm1_g=norm1_g,
                norm2_g=norm2_g, w_in=w_in, ret_decay=ret_decay, conv_w=conv_w, conv_b=conv_b,
                lru_wa=lru_wa, lru_ba=lru_ba, lru_wx=lru_wx, lru_bx=lru_bx, lru_lambda=lru_lambda,
                w_out=w_out, w_mlp1=w_mlp1, w_mlp2=w_mlp2, final_g=final_g)


def reference(x, c, ctx, c_ctx, w_ada, b_ada, norm1_g, norm2_g, w_in, ret_decay, conv_w, conv_b,
              lru_wa, lru_ba, lru_wx, lru_bx, lru_lambda, w_out, w_mlp1, w_mlp2, final_g):
    b, t_len, _ = x.shape
    cos, sin = axial_rotary_tables(t_len)
    k_scale = RET_HEAD_DIM ** -0.5
    silu_c = jax.nn.silu(c)
    silu_cc = jax.nn.silu(c_ctx)
    for l in range(DEPTH):
        last = l == DEPTH - 1
        mod = silu_c @ w_ada[l] + b_ada[l]
        mod_c = silu_cc @ w_ada[l] + b_ada[l]
        sh1, sc1, g1, sh2, sc2, g2 = [m[:, None] for m in jnp.split(mod, N_MOD, axis=-1)]
        csh1, csc1, cg1, csh2, csc2, cg2 = jnp.split(mod_c, N_MOD, axis=-1)
        w_q, w_k, w_v, w_g, w_x, w_gate = jnp.split(w_in[l], IN_SPLITS, axis=1)
        lg_f = jax.nn.log_sigmoid(ret_decay[l, 0].astype(jnp.float32))
        lg_b = jax.nn.log_sigmoid(ret_decay[l, 1].astype(jnp.float32))
        lru_f = (lru_wa[l, 0], lru_ba[l, 0], lru_wx[l, 0], lru_bx[l, 0], lru_lambda[l, 0])
        lru_b = (lru_wa[l, 1], lru_ba[l, 1], lru_wx[l, 1], lru_bx[l, 1], lru_lambda[l, 1])

        hc = modulate(rmsnorm(ctx, norm1_g[l]), csh1, csc1)
        kc = to_heads(hc @ w_k) * k_scale
        vc = to_heads(hc @ w_v)
        s_f = retention_final_state(kc, vc, lg_f, False)
        s_b = retention_final_state(kc, vc, lg_b, True)
        xcc = centred_depthwise_conv(hc @ w_x, conv_w[l], conv_b[l])
        zero_h = jnp.zeros((b, LRU_WIDTH), jnp.float32)
        hcf, lru_sf = rglru_direction(xcc, *lru_f, zero_h, False)
        hcb, lru_sb = rglru_direction(xcc, *lru_b, zero_h, True)
        if not last:
            qc = to_heads(hc @ w_q)
            zero_s = jnp.zeros((b, RET_HEADS, RET_HEAD_DIM, RET_HEAD_DIM), jnp.float32)
            o_c = retention_bidir(qc, kc, vc, lg_f, lg_b, zero_s, zero_s)
            yc = mix_output(o_c, hc @ w_g, hcf + hcb, hc @ w_gate, w_out[l])
            ctx_next = ctx + cg1 * yc
            hc2 = modulate(rmsnorm(ctx_next, norm2_g[l]), csh2, csc2)
            ctx_next = ctx_next + cg2 * squared_relu_mlp(hc2, w_mlp1[l], w_mlp2[l])

        h = modulate(rmsnorm(x, norm1_g[l]), sh1, sc1)
        q, k, v, g, xr, gate = jnp.split(h @ w_in[l], IN_SPLITS, axis=-1)
        q = apply_rotary(to_heads(q), cos, sin)
        k = apply_rotary(to_heads(k), cos, sin) * k_scale
        o = retention_bidir(q, k, to_heads(v), lg_f, lg_b, s_f, s_b)
        xcl = centred_depthwise_conv(xr, conv_w[l], conv_b[l])
        hf, _ = rglru_direction(xcl, *lru_f, lru_sf, False)
        hb, _ = rglru_direction(xcl, *lru_b, lru_sb, True)
        y = mix_output(o, g, hf + hb, gate, w_out[l])
        x = x + g1 * y
        h2 = modulate(rmsnorm(x, norm2_g[l]), sh2, sc2)
        x = x + g2 * squared_relu_mlp(h2, w_mlp1[l], w_mlp2[l])
        if not last:
            ctx = ctx_next
    return rmsnorm(x, final_g)
```

```python
import numpy as np
from contextlib import ExitStack
import concourse.bass as bass
import concourse.mybir as mybir
from concourse.bass_utils import run_bass_kernel_spmd

F32 = mybir.dt.float32
BF16 = mybir.dt.bfloat16
AF = mybir.ActivationFunctionType
ALU = mybir.AluOpType

ENGS = ["tensor", "vector", "scalar", "gpsimd", "sync"]
T = 4096
D = 1024
NCH = 32
TC = 256
EPS = 1e-6
KS = 128.0 ** -0.5
RD, N1, N2, CC, CX, CW, CB, BA, BX, LM, PC, NS = 0, 8, 16, 24, 32, 40, 56, 60, 68, 76, 84, 92


class Buf:
    __slots__ = ("name", "w", "r", "excl")

    def __init__(self, name="", excl=False):
        self.name = name
        self.w = None
        self.r = []
        self.excl = excl


class Prog:
    def __init__(self, nc, es):
        self.nc = nc
        self.es = es
        self.insts = []
        self.last_barrier = 0

    def op(self, eng, fn, reads=(), writes=(), dma_key=None, nobarrier=False):
        idx = len(self.insts)
        deps = set()
        xdeps = set()
        for b in reads:
            if b.excl:
                continue
            if b.w is not None:
                deps.add(b.w)
        for b in writes:
            if b.excl:
                continue
            if b.w is not None:
                deps.add(b.w)
            for r in b.r:
                deps.add(r)
        xb = [b for b in list(reads) + list(writes) if b.excl]
        for b in xb:
            if b.w is not None:
                xdeps.add(b.w)
            for r in b.r:
                xdeps.add(r)
        for d in xdeps:
            if self.insts[d]["eng"] != eng or self.insts[d]["dma_key"] is not None:
                deps.add(d)
        for b in reads:
            if not b.excl:
                b.r.append(idx)
        for b in writes:
            if not b.excl:
                b.w = idx
                b.r = []
        for b in xb:
            b.r = [r for r in b.r if self.insts[r]["eng"] != eng]
            if b.w is not None and self.insts[b.w]["eng"] != eng:
                b.r.append(b.w)
            b.w = idx
        deps.discard(idx)
        self.insts.append(dict(eng=eng, fn=fn, deps=deps, dma_key=dma_key, signal=False, nobarrier=nobarrier))
        return idx

    def barrier(self):
        last = {}
        dmas = []
        for i in range(len(self.insts)):
            ins = self.insts[i]
            if not ins.get("nobarrier"):
                last[ins["eng"]] = i
            if ins["dma_key"] is not None and i >= self.last_barrier and not ins.get("nobarrier"):
                dmas.append(i)
        deps = set(last.values()) | set(dmas)
        self.last_barrier = len(self.insts)
        for e in ENGS:
            idx = len(self.insts)
            self.insts.append(dict(eng=e, fn=lambda en: en.nop(), deps=set(deps), dma_key=None, signal=False))

    def finalize(self, block):
        nc, es = self.nc, self.es
        insts = self.insts
        all_dmas = set(i for i, ins in enumerate(insts) if ins["dma_key"] is not None)
        insts.append(dict(eng="sync", fn=lambda en: en.nop(), deps=all_dmas, dma_key=None, signal=False))
        for i, ins in enumerate(insts):
            keep = set()
            for d in ins["deps"]:
                p = insts[d]
                if p["dma_key"] is None and ins["dma_key"] is None and p["eng"] == ins["eng"]:
                    if p["eng"] == "tensor":
                        continue
                keep.add(d)
            ins["deps"] = keep
            for d in keep:
                insts[d]["signal"] = True
        sems, counts = {}, {}
        for e in ENGS:
            sems[e] = es.enter_context(nc.semaphore("s_" + e))
            counts[e] = 0
        for ins in insts:
            if ins["dma_key"] is not None:
                k = ins["dma_key"]
                if k not in sems:
                    sems[k] = es.enter_context(nc.semaphore("d%d" % len(sems)))
                    counts[k] = 0
                counts[k] += 16
                ins["sig"] = (sems[k], 16, counts[k], k)
            elif ins["signal"]:
                e = ins["eng"]
                counts[e] += 1
                ins["sig"] = (sems[e], 1, counts[e], e)
            else:
                ins["sig"] = None
        self.n_sems = len(sems)
        seen = {e: {} for e in ENGS}
        streams = {e: [] for e in ENGS}
        for ins in insts:
            e = ins["eng"]
            waits = {}
            for d in ins["deps"]:
                sem, inc, val, key = insts[d]["sig"]
                if seen[e].get(key, 0) >= val:
                    continue
                if key not in waits or waits[key][1] < val:
                    waits[key] = (sem, val)
            for key, (sem, val) in waits.items():
                seen[e][key] = val
            streams[e].append((list(waits.values()), ins))

        def make(e):
            def body(eng):
                for waits, ins in streams[e]:
                    for sem, val in waits:
                        eng.wait_ge(sem, val)
                    r = ins["fn"](eng)
                    if ins["sig"] is not None:
                        r.then_inc(ins["sig"][0], ins["sig"][1])
            return body

        for e in ENGS:
            if streams[e]:
                getattr(block, e)(make(e))


class Arena:
    def __init__(self, nc, es, nbytes):
        self.t = es.enter_context(nc.sbuf_tensor("arena", [128, nbytes // 4], F32))
        self.top = 0
        self.cap = nbytes
        self.peak = 0

    def alloc(self, shape, dt, parts=128):
        esz = 4 if dt == F32 else 2
        n = int(np.prod(shape))
        nb = (n * esz + 63) // 64 * 64
        off = self.top
        self.top += nb
        self.peak = max(self.peak, self.top)
        assert self.top <= self.cap, ("SBUF arena overflow", self.top, self.cap)
        ap = self.t[0:parts, off // 4:(off + nb) // 4]
        if dt != F32:
            ap = ap.bitcast(dt)
        ap = ap[:, 0:n]
        if len(shape) == 2:
            ap = ap.rearrange("p (a b) -> p a b", a=shape[0], b=shape[1])
        elif len(shape) == 3:
            ap = ap.rearrange("p (a b c) -> p a b c", a=shape[0], b=shape[1], c=shape[2])
        return ap


def build_program(debug=None):
    nc = bass.Bass("TRN2", target_bir_lowering=False)

    def dram(n, s, dt=F32, kind="ExternalInput"):
        return nc.dram_tensor(n, s, dt, kind=kind).ap()

    x_d = dram("x", [T, D])
    ctx_d = dram("ctx", [TC, D])
    smalls_d = dram("smalls", [128, NS])
    wada_d = dram("w_ada", [D, 6 * D])
    bada_d = dram("b_ada", [1, 6 * D])
    win_d = dram("w_in", [D, 3072])
    wout_d = dram("w_out", [D, D])
    w1_d = dram("w_mlp1", [D, 4096])
    w2_d = dram("w_mlp2", [4096, D])
    fg_d = dram("final_g_rep", [128, D])
    wabd_d = dram("wabd", [128, 1024])
    wxbd_d = dram("wxbd", [128, 1024])
    ident_d = dram("ident", [128, 128])
    mk_d = dram("mk", [128, 6 * 128])
    rot_d = dram("rot", [NCH, 128, 128])
    out_d = dram("out", [T, D], kind="ExternalOutput")
    w1s = dram("w1s", [8, 128, 4096], BF16, kind="Internal")
    w2s = dram("w2s", [8, 128, 4096], BF16, kind="Internal")
    wos = dram("wos", [2, 128, 4096], BF16, kind="Internal")
    gsc = dram("gsc", [2, D], F32, kind="Internal")
    wqs = dram("wqs", [128, 8, 2048], BF16, kind="Internal")
    dbg_d = None
    if debug is not None:
        dbg_d = dram("dbg", list(debug[1]), F32, kind="ExternalOutput")

    es = ExitStack()
    with es:
        A = Arena(nc, es, 212480)
        psAll = es.enter_context(nc.psum_tensor("psAll", [128, 4096], F32))
        block = es.enter_context(nc.Block())
        P = Prog(nc, es)
        Bps = [Buf("ps%d" % i, excl=True) for i in range(8)]

        def bank(b):
            return psAll[:, b * 512:(b + 1) * 512]

        def bankb(b):
            return psAll[:, b * 512:(b + 1) * 512].bitcast(BF16)

        def MM(out, lhsT, rhs, start, stop, r, w):
            P.op("tensor", lambda e: e.matmul(out, lhsT=lhsT, rhs=rhs, start=start, stop=stop), r, w)

        def TR(out, in_, ident, r, w):
            P.op("tensor", lambda e: e.transpose(out=out, in_=in_, identity=ident), r, w)

        def ACT(out, in_, func, r, w, scale=None, bias=None, accum=None):
            kw = {}
            if scale is not None:
                kw["scale"] = scale
            if bias is not None:
                kw["bias"] = bias
            if accum is not None:
                kw["accum_out"] = accum
            P.op("scalar", lambda e: e.activation(out=out, in_=in_, func=func, **kw), r, w)

        def TT(out, in0, in1, op, r, w, eng="vector"):
            P.op(eng, lambda e: e.tensor_tensor(out=out, in0=in0, in1=in1, op=op), r, w)

        def TS(out, in0, s1, s2, op0, op1, r, w, eng="vector"):
            if op1 is None:
                P.op(eng, lambda e: e.tensor_scalar(out=out, in0=in0, scalar1=s1, scalar2=None, op0=op0), r, w)
            else:
                P.op(eng, lambda e: e.tensor_scalar(out=out, in0=in0, scalar1=s1, scalar2=s2, op0=op0, op1=op1), r, w)

        def STT(out, in0, scalar, in1, op0, op1, r, w):
            P.op("vector", lambda e: e.scalar_tensor_tensor(out=out, in0=in0, scalar=scalar, in1=in1, op0=op0, op1=op1), r, w)

        def CP(out, in_, r, w, eng="vector"):
            P.op(eng, lambda e: e.tensor_copy(out=out, in_=in_), r, w)

        def MEMSET(ap, val, w, eng="vector"):
            P.op(eng, lambda e: e.memset(ap, val), (), w)

        def DMA(eng, out, in_, r, w, key, nobarrier=False):
            P.op(eng, lambda e: e.dma_start(out=out, in_=in_), r, w, dma_key=key, nobarrier=nobarrier)

        B_hT = [Buf("hT%d" % i) for i in range(NCH)]
        B_lruT = [[Buf("lruT") for _ in range(8)] for _ in range(4)]
        B_ret = [Buf("ret%d" % i) for i in range(NCH)]
        smalls = A.alloc([NS], F32)
        identF = A.alloc([128], F32)
        identB = A.alloc([128], BF16)
        lg = A.alloc([8], F32)
        cch = A.alloc([8], F32)
        cch2 = A.alloc([8], F32)
        posw = A.alloc([8, 4], F32)
        Gd = A.alloc([8], F32)
        Dm = A.alloc([4, 128], BF16)
        WQT = A.alloc([2, 4, 128], F32)
        scl = A.alloc([3, 8], F32)
        fm = A.alloc([6, 8], F32)
        lru_s0 = A.alloc([2, 4], F32)
        SfRun = A.alloc([512], F32)
        SbRun = A.alloc([512], F32)
        zcol = A.alloc([1], F32)
        stats = A.alloc([8, 4], F32)
        B_small = Buf("small")
        B_stats = [Buf("st%d" % i) for i in range(8)]
        B_Sf, B_Sb = Buf("Sf"), Buf("Sb")
        B_s0 = Buf("s0")
        hT_off = A.top
        hT = A.alloc([8, T], BF16)
        lruT_off = A.top
        lruT = A.alloc([4, T], BF16)
        mark_persist = A.top
        hcT = A.alloc([8, TC], BF16)
        wabd = A.alloc([2, 4, 128], BF16)
        wxbd = A.alloc([2, 4, 128], BF16)
        B_hcT, B_wbd = Buf("hcT"), Buf("wbd")
        mark_A = A.top

        B_w1s = [Buf("w1s%d" % u) for u in range(8)]
        B_w2s = [Buf("w2s%d" % u) for u in range(8)]
        B_wos = [Buf("wos%d" % u) for u in range(2)]

        conv_pieces = []
        conv_last = {}

        def build_weight_conversion():
            w1v = w1_d.rearrange("(kc p) (u j) -> u p kc j", p=128, j=512)
            for u in range(8):
                dst = w1s[u].rearrange("p (kc j) -> p kc j", kc=8)
                for hh in range(2):
                    conv_pieces.append((dst[:, hh * 4:(hh + 1) * 4, :], w1v[u][:, hh * 4:(hh + 1) * 4, :], B_w1s[u], "cv_w1"))
            w2v = w2_d.rearrange("(u hb p) (fh j) -> fh u p hb j", p=128, hb=8, j=512)
            for fh in range(2):
                for u4 in range(4):
                    dst = w2s[fh * 4 + u4].rearrange("p (hb j) -> p hb j", hb=8)
                    for hh in range(2):
                        conv_pieces.append((dst[:, hh * 4:(hh + 1) * 4, :], w2v[fh, u4][:, hh * 4:(hh + 1) * 4, :],
                                            B_w2s[fh * 4 + u4], "cv_w2"))
            wov = wout_d.rearrange("(u kc p) n -> u p kc n", p=128, kc=4)
            for u in range(2):
                dst = wos[u].rearrange("p (kc n) -> p kc n", kc=4)
                for hh in range(2):
                    conv_pieces.append((dst[:, hh * 2:(hh + 1) * 2, :], wov[u][:, hh * 2:(hh + 1) * 2, :], B_wos[u], "cv_wo"))

        def emit_conv_piece():
            if conv_pieces:
                dst, src, Bd, key = conv_pieces.pop(0)
                P.op("gpsimd", lambda e, dst=dst, src=src: e.dma_start(out=dst, in_=src), (), (), dma_key=key, nobarrier=True)
                conv_last[key] = len(P.insts) - 1

        DMA("sync", smalls, smalls_d, (), [B_small], "c_small")
        DMA("sync", identF, ident_d, (), [B_small], "c_small")
        DMA("gpsimd", identB, ident_d, (), [B_small], "c_identb")
        DMA("gpsimd", wabd.rearrange("p a b c -> p (a b c)"), wabd_d, (), [B_wbd], "c_wbd")
        DMA("gpsimd", wxbd.rearrange("p a b c -> p (a b c)"), wxbd_d, (), [B_wbd], "c_wbd")

        A.top = lruT_off
        wada_sl = [A.alloc([8, 1024], BF16) for _ in range(2)]
        assert A.top <= mark_persist
        A.top = mark_A
        wxg = A.alloc([8, 1024], BF16)
        B_wxg = Buf()
        mark_A2 = A.top
        NX1 = 3
        xs1 = [A.alloc([D], F32) for _ in range(NX1)]
        xn1 = [A.alloc([D], BF16) for _ in range(3)]
        junk1 = A.alloc([D], BF16)
        B_xs1 = [Buf() for _ in range(NX1)]
        B_xn1 = [Buf() for _ in range(3)]
        mk = A.alloc([6, 128], F32)
        tmpA = A.alloc([128], F32)
        tmpB = A.alloc([128], F32)
        tmp8 = A.alloc([4, 8], F32)
        onesF = A.alloc([128], F32)
        ones1 = A.alloc([128], BF16, parts=1)
        bada_sl = [A.alloc([1024], BF16, parts=1) for _ in range(2)]
        silc = A.alloc([2, 8], F32)
        crep = A.alloc([2, 8, 128], BF16)
        mblk = [A.alloc([512], F32) for _ in range(4)]
        B_mblk = [Buf() for _ in range(4)]
        wkv = A.alloc([8, 1024], BF16)
        xct = [A.alloc([D], F32) for _ in range(2)]
        xcn = [A.alloc([D], BF16) for _ in range(2)]
        kct = A.alloc([2, 512], BF16)
        vcw = A.alloc([2, 2, 512], BF16)
        B_mk, B_tA, B_tB, B_t8, B_ones = Buf(), Buf(), Buf(), Buf(), Buf()
        B_badab = [Buf(), Buf()]
        B_silc, B_crep, B_wkv = Buf(), Buf(), Buf()
        B_wada = [Buf(), Buf()]
        B_xct, B_xcn = [Buf(), Buf()], [Buf(), Buf()]
        B_kct, B_vcw = Buf(), Buf()
        B_fm = Buf("fm")
        B_fm1 = Buf("fm1")

        DMA("sync", mk.rearrange("p a b -> p (a b)"), mk_d, (), [B_mk], "c_mk")
        wadav = wada_d.rearrange("(kc p) n -> p kc n", p=128)
        for q in range(2):
            DMA("gpsimd", wada_sl[q], wadav[:, :, q * 1024:(q + 1) * 1024], (), [B_wada[q]], "c_wada%d" % q)
            DMA("gpsimd", bada_sl[q], bada_d[:, q * 1024:(q + 1) * 1024], (), [B_badab[q]], "c_bada%d" % q)
        winv = win_d.rearrange("(kc p) n -> p kc n", p=128)
        DMA("gpsimd", wkv, winv[:, :, 512:1536], (), [B_wkv], "c_wkv")
        DMA("gpsimd", wxg, winv[:, :, 2048:3072], (), [B_wxg], "c_wxg")
        for t in range(2):
            DMA("sync", xct[t], ctx_d[t * 128:(t + 1) * 128, :], (), [B_xct[t]], "c_ctx%d" % t)

        MEMSET(onesF, 1.0, [B_ones])
        MEMSET(ones1, 1.0, [B_ones])
        MEMSET(zcol, 0.0, [B_small])
        ACT(tmp8[:, 0, :], smalls[:, RD:RD + 8], AF.Sigmoid, [B_small], [B_t8])
        ACT(tmp8[:, 1, :], smalls[:, LM:LM + 8], AF.Sigmoid, [B_small], [B_t8])
        ACT(tmp8[:, 2, :], smalls[:, CC:CC + 8], AF.Silu, [B_small], [B_t8])
        ACT(tmp8[:, 3, :], smalls[:, CX:CX + 8], AF.Silu, [B_small], [B_t8])
        ACT(lg, tmp8[:, 0, :], AF.Ln, [B_t8], [B_small])
        ACT(cch, tmp8[:, 1, :], AF.Ln, [B_t8], [B_small])
        TS(cch2, cch, 16.0, None, ALU.mult, None, [B_small], [B_small])
        TS(cch, cch, 8.0, None, ALU.mult, None, [B_small], [B_small])
        for j in range(8):
            dr = 0 if j in (0, 2, 4, 5) else 1
            TS(posw[:, j, :], lg[:, dr * 4:(dr + 1) * 4], smalls[:, PC + j:PC + j + 1], None, ALU.mult, None,
               [B_small], [B_small])
        ACT(posw.rearrange("p a b -> p (a b)"), posw.rearrange("p a b -> p (a b)"), AF.Exp, [B_small], [B_small])
        ACT(Gd, lg, AF.Exp, [B_small], [B_small], scale=128.0)
        TS(posw[:, 0:2, :], posw[:, 0:2, :], KS, None, ALU.mult, None, [B_small], [B_small])
        TS(posw[:, 4:8, :], posw[:, 4:8, :], KS, None, ALU.mult, None, [B_small], [B_small])
        for h in range(4):
            ACT(tmpA, mk[:, 0, :], AF.Exp, [B_mk, B_small], [B_tA], scale=lg[:, h:h + 1])
            TT(tmpA, tmpA, mk[:, 1, :], ALU.mult, [B_mk, B_tA], [B_tA])
            ACT(tmpB, mk[:, 2, :], AF.Exp, [B_mk, B_small], [B_tB], scale=lg[:, 4 + h:5 + h])
            TT(tmpB, tmpB, mk[:, 3, :], ALU.mult, [B_mk, B_tB], [B_tB])
            TT(tmpA, tmpA, tmpB, ALU.add, [B_tA, B_tB], [B_tA])
            TS(Dm[:, h, :], tmpA, KS, None, ALU.mult, None, [B_tA], [B_small])
            ACT(WQT[:, 0, h, :], mk[:, 4, :], AF.Exp, [B_mk, B_small], [B_small], scale=lg[:, h:h + 1])
            ACT(WQT[:, 1, h, :], mk[:, 5, :], AF.Exp, [B_mk, B_small], [B_small], scale=lg[:, 4 + h:5 + h])
        for v in range(2):
            for kc in range(8):
                TS(crep[:, v, kc, :], onesF, tmp8[:, 2 + v, kc:kc + 1], None, ALU.mult, None, [B_ones, B_t8], [B_crep])
        stat_ctr = [0]

        def norm_pre(src, Bsrc, xn, Bxn):
            s = stat_ctr[0] % 8
            stat_ctr[0] += 1
            st, Bst = stats[:, s, :], B_stats[s]
            ACT(xn, src, AF.Square, [Bsrc], [Bxn, Bst], accum=st[:, 0:1])
            ACT(st[:, 1:2], st[:, 0:1], AF.Sqrt, [Bst], [Bst], scale=1.0 / D, bias=EPS)
            P.op("vector", lambda e: e.reciprocal(out=st[:, 2:3], in_=st[:, 1:2]), [Bst], [Bst])
            ACT(xn, src, AF.Identity, [Bsrc, Bst], [Bxn], scale=st[:, 2:3])

        def norm_post(xn, Bxn, sidx, bidx, dst_fn, Bdst, tbank):
            pb = bankb(tbank).rearrange("p (a b) -> p a b", a=8)
            for kc in range(8):
                TR(pb[:, kc, :], xn[:, kc * 128:(kc + 1) * 128], identB, [Bxn, B_small], [Bps[tbank]])
            for kc in range(8):
                if False:
                    ACT(dst_fn(kc), pb[:, kc, :], AF.Identity, [Bps[tbank], B_fm], [Bdst],
                        scale=scl[:, sidx, kc:kc + 1], bias=fm[:, bidx, kc:kc + 1])
                else:
                    TS(dst_fn(kc), pb[:, kc, :], scl[:, sidx, kc:kc + 1], fm[:, bidx, kc:kc + 1], ALU.mult, ALU.add,
                       [Bps[tbank], B_fm1], [Bdst])

        def norm_T(src, Bsrc, xn, Bxn, sidx, bidx, dst_fn, Bdst, tbank):
            norm_pre(src, Bsrc, xn, Bxn)
            norm_post(xn, Bxn, sidx, bidx, dst_fn, Bdst, tbank)

        p1_st = {}

        def p1_stats(tt):
            s_ = tt % NX1
            DMA("sync", xs1[s_], x_d[tt * 128:(tt + 1) * 128, :], (), [B_xs1[s_]], "x1_%d" % s_)
            k = stat_ctr[0] % 8
            stat_ctr[0] += 1
            st, Bst = stats[:, k, :], B_stats[k]
            ACT(junk1, xs1[s_], AF.Square, [B_xs1[s_]], [Bst], accum=st[:, 0:1])
            ACT(st[:, 1:2], st[:, 0:1], AF.Sqrt, [Bst], [Bst], scale=1.0 / D, bias=EPS)
            P.op("vector", lambda e, st=st: e.reciprocal(out=st[:, 2:3], in_=st[:, 1:2]), [Bst], [Bst])
            p1_st[tt] = (st, Bst)

        def p1_apply(tt):
            s_ = tt % NX1
            st, Bst = p1_st[tt]
            ACT(xn1[tt % 3], xs1[s_], AF.Identity, [B_xs1[s_], Bst], [B_xn1[tt % 3]], scale=st[:, 2:3])

        def p1_post(tt):
            norm_post(xn1[tt % 3], B_xn1[tt % 3], 0, 0, lambda kc, tt=tt: hT[:, kc, tt * 128:(tt + 1) * 128], B_hT[tt], tt % 2)

        def emit_phase1():
            for tt in range(NCH + 2):
                if tt < NCH:
                    p1_stats(tt)
                if 1 <= tt <= NCH:
                    p1_apply(tt - 1)
                if tt >= 2:
                    p1_post(tt - 2)


        B_gsc = Buf("gsc")
        mb_ctr = [0]
        vi_of = {0: 0, 1: 1, 3: 2, 4: 3}
        for q in range(6):
            sl = q % 2
            if q >= 2:
                DMA("gpsimd", wada_sl[sl], wadav[:, :, q * 1024:(q + 1) * 1024], (), [B_wada[sl]], "c_wada%d" % sl)
                DMA("gpsimd", bada_sl[sl], bada_d[:, q * 1024:(q + 1) * 1024], (), [B_badab[sl]], "c_bada%d" % sl)
            for j in range(2):
                J = 2 * q + j
                vo, half = J // 2, J % 2
                for v in ((0, 1) if vo < 2 else (0,)):
                    ms = mb_ctr[0] % 4
                    mb_ctr[0] += 1
                    bk = ms
                    for kc in range(8):
                        MM(bank(bk), crep[:, v, kc, :], wada_sl[sl][:, kc, j * 512:(j + 1) * 512], kc == 0, False,
                           [B_crep, B_wada[sl]], [Bps[bk]])
                    MM(bank(bk), ones1[0:1, :], bada_sl[sl][0:1, j * 512:(j + 1) * 512], False, True,
                       [B_ones, B_badab[sl]], [Bps[bk]])
                    ACT(mblk[ms], bank(bk), AF.Identity, [Bps[bk]], [B_mblk[ms]])
                    if vo in (2, 5):
                        gi = 0 if vo == 2 else 1
                        DMA("sync", gsc[gi:gi + 1, half * 512:(half + 1) * 512], mblk[ms][0:1, :], [B_mblk[ms]], [B_gsc], "c_gsc")
                    else:
                        vi = vi_of[vo] if v == 0 else 4 + vo
                        tb = 4 + ms
                        for k4 in range(4):
                            TR(bank(tb)[:, k4 * 128:(k4 + 1) * 128], mblk[ms][:, k4 * 128:(k4 + 1) * 128], identF,
                               [B_mblk[ms], B_small], [Bps[tb]])
                        CP(fm[:, vi, half * 4:(half + 1) * 4], bank(tb).rearrange("p (a b) -> p a b", a=4)[:, :, 0], [Bps[tb]], [B_fm])
            if q == 1:
                STT(scl[:, 0, :], fm[:, 1, :], 1.0, smalls[:, N1:N1 + 8], ALU.add, ALU.mult, [B_fm, B_small], [B_fm1])
                STT(scl[:, 2, :], fm[:, 5, :], 1.0, smalls[:, N1:N1 + 8], ALU.add, ALU.mult, [B_fm, B_small], [B_fm1])
                emit_phase1()
        STT(scl[:, 1, :], fm[:, 3, :], 1.0, smalls[:, N2:N2 + 8], ALU.add, ALU.mult, [B_fm, B_small], [B_fm1])

        for t in range(2):
            norm_T(xct[t], B_xct[t], xcn[t], B_xcn[t], 2, 4, lambda kc, t=t: hcT[:, kc, t * 128:(t + 1) * 128], B_hcT, 6 + t)
        for t in range(2):
            for kc in range(8):
                MM(bank(0), hcT[:, kc, t * 128:(t + 1) * 128], wkv[:, kc, 0:512], kc == 0, kc == 7, [B_hcT, B_wkv], [Bps[0]])
            for kc in range(8):
                MM(bank(1), hcT[:, kc, t * 128:(t + 1) * 128], wkv[:, kc, 512:1024], kc == 0, kc == 7, [B_hcT, B_wkv], [Bps[1]])
            ACT(kct[:, t, :], bank(0), AF.Identity, [Bps[0]], [B_kct])
            for dr in range(2):
                for h in range(4):
                    j = 4 + 2 * dr + t
                    sc = posw[:, j, h:h + 1]
                    if (dr + h) % 2 == 0:
                        ACT(vcw[:, t, dr, h * 128:(h + 1) * 128], bank(1)[:, h * 128:(h + 1) * 128], AF.Identity,
                            [Bps[1], B_small], [B_vcw], scale=sc)
                    else:
                        TS(vcw[:, t, dr, h * 128:(h + 1) * 128], bank(1)[:, h * 128:(h + 1) * 128], sc, None, ALU.mult, None,
                           [Bps[1], B_small], [B_vcw])
        for dr in range(2):
            bk = 2 + dr
            for h in range(4):
                for t in range(2):
                    MM(bank(bk)[:, h * 128:(h + 1) * 128], kct[:, t, h * 128:(h + 1) * 128], vcw[:, t, dr, h * 128:(h + 1) * 128],
                       t == 0, t == 1, [B_kct, B_vcw], [Bps[bk]])
            if dr == 0:
                ACT(SfRun, bank(bk), AF.Identity, [Bps[bk]], [B_Sf])
            else:
                CP(SbRun, bank(bk), [Bps[bk]], [B_Sb])

        if debug is not None and debug[0] == "ctx":
            Bd = Buf()
            DMA("sync", dbg_d[:, 0:512], SfRun, [B_Sf], [Bd], "dbg")
            DMA("sync", dbg_d[:, 512:1024], SbRun, [B_Sb], [Bd], "dbg")
            DMA("gpsimd", dbg_d[:, 1024:3072], hcT.rearrange("p a b -> p (a b)"), [B_hcT], [Bd], "dbg")
            DMA("sync", dbg_d[:, 3072:3072 + 48], fm.rearrange("p a b -> p (a b)"), [B_fm], [Bd], "dbg")
            DMA("sync", dbg_d[:, 3120:3120 + 24], scl.rearrange("p a b -> p (a b)"), [B_fm], [Bd], "dbg")
            P.op("sync", lambda e: e.nop(), [Bd], ())
            P.finalize(block)
            return nc, A

        if debug is not None and debug[0] == "hT":
            Bd = Buf()
            DMA("gpsimd", dbg_d, hT.rearrange("p a b -> p (a b)"), B_hT, [Bd], "dbg")
            P.op("sync", lambda e: e.nop(), [Bd], ())
            P.finalize(block)
            return nc, A

        P.barrier()
        A.top = mark_A2
        B_wqs = Buf("wqs")
        DMA("gpsimd", wqs, winv[:, :, 0:2048], (), [B_wqs], "cv_wq", nobarrier=True)
        xc = A.alloc([T], F32)
        xcb = A.alloc([T], BF16)
        acc = A.alloc([T], F32)
        QS = 512
        Rb = [[A.alloc([QS], F32) for _ in range(2)] for _ in range(2)]
        Ib = [[A.alloc([QS], F32) for _ in range(2)] for _ in range(2)]
        Qb = [[A.alloc([QS], F32) for _ in range(2)] for _ in range(2)]
        Hb = [[A.alloc([QS], F32) for _ in range(2)] for _ in range(2)]
        gl = [A.alloc([512], F32) for _ in range(2)]
        carry = A.alloc([2], F32)
        B_xc, B_carry = Buf(), [Buf(), Buf()]
        B_xcbq = [Buf() for _ in range(8)]
        B_acc = [Buf() for _ in range(8)]
        B_R = [[Buf(), Buf()], [Buf(), Buf()]]
        B_I = [[Buf(), Buf()], [Buf(), Buf()]]
        B_Q = [[Buf(), Buf()], [Buf(), Buf()]]
        B_H = [[Buf(), Buf()], [Buf(), Buf()]]
        B_gl = [Buf(), Buf()]

        def lru_stage(Tn, srcT, B_src_of, is_ctx):
            nsl = max(1, Tn // 512)
            W = min(512, Tn)
            NQ = Tn // W
            def emit_xr(ct_, s_):
                for kc in range(8):
                    MM(bank(s_)[:, 0:W], wxg[:, kc, ct_ * 128:(ct_ + 1) * 128], srcT[:, kc, s_ * 512:s_ * 512 + W], kc == 0, kc == 7,
                       [B_wxg] + B_src_of(s_), [Bps[s_]])

            xr_done = set()
            for ct in range(4):
                for s_ in range(nsl):
                    if (ct, s_) not in xr_done:
                        emit_xr(ct, s_)
                pr = list(Bps[0:nsl])
                cw = lambda j: smalls[:, CW + ct * 4 + j:CW + ct * 4 + j + 1]
                cb = smalls[:, CB + ct:CB + ct + 1]
                TS(xc[:, 1:Tn], psAll[:, 0:Tn - 1], cw(0), cb, ALU.mult, ALU.add, pr + [B_small], [B_xc])
                ACT(xc[:, 0:1], zcol, AF.Identity, [B_small], [B_xc], scale=1.0, bias=cb)
                STT(xc[:, 0:Tn], psAll[:, 0:Tn], cw(1), xc[:, 0:Tn], ALU.mult, ALU.add, pr + [B_small, B_xc], [B_xc])
                STT(xc[:, 0:Tn - 1], psAll[:, 1:Tn], cw(2), xc[:, 0:Tn - 1], ALU.mult, ALU.add, pr + [B_small, B_xc], [B_xc])
                STT(xc[:, 0:Tn - 2], psAll[:, 2:Tn], cw(3), xc[:, 0:Tn - 2], ALU.mult, ALU.add, pr + [B_small, B_xc], [B_xc])
                order = []
                for j in range(NQ):
                    for q_ in (j, NQ - 1 - j):
                        if q_ not in order:
                            order.append(q_)
                for oi, q_ in enumerate(order):
                    if oi % 4 == 3:
                        ACT(xcb[:, q_ * W:(q_ + 1) * W], xc[:, q_ * W:(q_ + 1) * W], AF.Identity, [B_xc], [B_xcbq[q_]])
                    else:
                        CP(xcb[:, q_ * W:(q_ + 1) * W], xc[:, q_ * W:(q_ + 1) * W], [B_xc], [B_xcbq[q_]])
                for j in range(NQ):
                    st_ = j % 2
                    qd = (j, NQ - 1 - j)
                    for dr in range(2):
                        t0 = qd[dr] * W
                        for (k2, wsrc, dst, Bd, bcol) in ((0, wabd, Rb, B_R, BA), (1, wxbd, Ib, B_I, BX)):
                            bk = st_ * 4 + dr * 2 + k2
                            MM(bank(bk)[:, 0:W], wsrc[:, dr, ct, :], xcb[:, t0:t0 + W], True, True, [B_wbd, B_xcbq[qd[dr]]], [Bps[bk]])
                            ACT(dst[st_][dr][:, 0:W], bank(bk)[:, 0:W], AF.Sigmoid, [Bps[bk], B_small], [Bd[st_][dr]],
                                bias=smalls[:, bcol + dr * 4 + ct:bcol + dr * 4 + ct + 1])
                    for dr in range(2):
                        R, I, Q = Rb[st_][dr][:, 0:W], Ib[st_][dr][:, 0:W], Qb[st_][dr][:, 0:W]
                        ccol = cch[:, dr * 4 + ct:dr * 4 + ct + 1]
                        ACT(R, R, AF.Exp, [B_R[st_][dr], B_small], [B_R[st_][dr]], scale=ccol)
                    for dr in range(2):
                        R, I, Q = Rb[st_][dr][:, 0:W], Ib[st_][dr][:, 0:W], Qb[st_][dr][:, 0:W]
                        t0 = qd[dr] * W
                        TT(Q, R, R, ALU.mult, [B_R[st_][dr]], [B_Q[st_][dr]], eng="gpsimd")
                        TT(I, I, xc[:, t0:t0 + W], ALU.mult, [B_I[st_][dr], B_xc], [B_I[st_][dr]], eng="gpsimd")
                    for dr in range(2):
                        Q = Qb[st_][dr][:, 0:W]
                        ACT(Q, Q, AF.Ln, [B_Q[st_][dr]], [B_Q[st_][dr]], scale=-1.0, bias=1.0)
                    for dr in range(2):
                        Q = Qb[st_][dr][:, 0:W]
                        ACT(Q, Q, AF.Exp, [B_Q[st_][dr]], [B_Q[st_][dr]], scale=0.5)
                    for dr in range(2):
                        R, I, Q, H = Rb[st_][dr][:, 0:W], Ib[st_][dr][:, 0:W], Qb[st_][dr][:, 0:W], Hb[st_][dr][:, 0:W]
                        t0 = qd[dr] * W
                        TT(I, I, Q, ALU.mult, [B_I[st_][dr], B_Q[st_][dr]], [B_I[st_][dr]], eng="gpsimd")
                        if j == 0:
                            init = zcol if is_ctx else lru_s0[:, dr, ct:ct + 1]
                            Binit = B_small if is_ctx else B_s0
                        else:
                            init = carry[:, dr:dr + 1]
                            Binit = B_carry[dr]
                        if dr == 0:
                            P.op("vector", lambda e, H=H, R=R, I=I, init=init: e.tensor_tensor_scan(
                                out=H, data0=R, data1=I, initial=init, op0=ALU.mult, op1=ALU.add),
                                [B_R[st_][dr], B_I[st_][dr], Binit], [B_H[st_][dr]])
                            CP(carry[:, 0:1], H[:, W - 1:W], [B_H[st_][dr]], [B_carry[0]])
                        else:
                            P.op("vector", lambda e, H=H, R=R, I=I, init=init: e.tensor_tensor_scan(
                                out=H[:, ::-1], data0=R[:, ::-1], data1=I[:, ::-1], initial=init, op0=ALU.mult, op1=ALU.add),
                                [B_R[st_][dr], B_I[st_][dr], Binit], [B_H[st_][dr]])
                            CP(carry[:, 1:2], H[:, 0:1], [B_H[st_][dr]], [B_carry[1]])
                        if not is_ctx:
                            q_ = qd[dr]
                            other_step = NQ - 1 - j
                            first = j < other_step or (j == other_step and dr == 0)
                            if first:
                                CP(acc[:, t0:t0 + W], H, [B_H[st_][dr]], [B_acc[q_]])
                            else:
                                TT(acc[:, t0:t0 + W], acc[:, t0:t0 + W], H, ALU.add, [B_H[st_][dr], B_acc[q_]], [B_acc[q_]])
                        elif j == NQ - 1:
                            if dr == 0:
                                CP(lru_s0[:, 0, ct:ct + 1], H[:, W - 1:W], [B_H[st_][dr]], [B_s0])
                            else:
                                CP(lru_s0[:, 1, ct:ct + 1], H[:, 0:1], [B_H[st_][dr]], [B_s0])
                if not is_ctx:
                    for sg in range(8):
                        bk = 6 + (sg % 2)
                        for kc in range(8):
                            MM(bank(bk), wxg[:, kc, 512 + ct * 128:512 + (ct + 1) * 128], srcT[:, kc, sg * 512:(sg + 1) * 512],
                               kc == 0, kc == 7, [B_wxg] + B_src_of(sg), [Bps[bk]])
                        g = gl[sg % 2]
                        ACT(g, bank(bk), AF.Gelu, [Bps[bk]], [B_gl[sg % 2]])
                        TT(lruT[:, ct, sg * 512:(sg + 1) * 512], acc[:, sg * 512:(sg + 1) * 512], g, ALU.mult,
                           [B_acc[sg], B_gl[sg % 2]], [B_lruT[ct][sg]])
                        if ct + 1 < 4 and sg < 6:
                            emit_xr(ct + 1, sg)
                            xr_done.add((ct + 1, sg))

        lru_stage(TC, hcT, lambda s: [B_hcT], True)
        lru_stage(T, hT, lambda s: B_hT[s * 4:(s + 1) * 4], False)

        if debug is not None and debug[0] == "lru":
            Bd = Buf()
            DMA("gpsimd", dbg_d[:, 0:4 * T], lruT.rearrange("p a b -> p (a b)"), [b for l in B_lruT for b in l], [Bd], "dbg")
            DMA("sync", dbg_d[:, 4 * T:4 * T + 8], lru_s0.rearrange("p a b -> p (a b)"), [B_s0], [Bd], "dbg")
            P.op("sync", lambda e: e.nop(), [Bd], ())
            P.finalize(block)
            return nc, A

        P.barrier()
        A.top = mark_persist
        Sb = A.alloc([NCH, 512], BF16)
        B_Sbn = [Buf() for _ in range(NCH)]
        wq = A.alloc([8, 2048], BF16)
        B_wq_kv, B_wq_qg = Buf(), Buf()
        B_wq = [B_wq_kv] * 8
        DMA("sync", wq[:, :, 512:1536], wqs[:, :, 512:1536], [B_wqs], [B_wq_kv], "c_wqkv")
        DMA("sync", wq[:, :, 0:512], wqs[:, :, 0:512], [B_wqs], [B_wq_qg], "c_wqqg")
        DMA("sync", wq[:, :, 1536:2048], wqs[:, :, 1536:2048], [B_wqs], [B_wq_qg], "c_wqqg")
        build_weight_conversion()
        rot = [A.alloc([128], F32) for _ in range(3)]
        B_rot = [Buf() for _ in range(3)]
        t1 = [A.alloc([512], F32) for _ in range(2)]
        t2 = [A.alloc([512], F32) for _ in range(2)]
        B_t1, B_t2 = [Buf(), Buf()], [Buf(), Buf()]
        qr = [A.alloc([512], BF16) for _ in range(2)]
        kr = [A.alloc([512], BF16) for _ in range(2)]
        vv = [A.alloc([512], BF16) for _ in range(2)]
        vw = [A.alloc([512], BF16) for _ in range(2)]
        sgt = [A.alloc([512], F32) for _ in range(2)]
        kqT = A.alloc([4, 4, 128], BF16)
        sT = A.alloc([4, 128], BF16)
        on = A.alloc([512], F32)
        retb = [A.alloc([512], BF16) for _ in range(2)]
        SfB = [A.alloc([512], BF16) for _ in range(2)]
        t1x = A.alloc([512], F32)
        SfRun2 = t1x
        SbRun2 = t1x
        B_Sf2 = Buf()
        B_Sb2 = B_Sf2
        bnst = A.alloc([4, 6], F32)
        mv = A.alloc([4, 2], F32)
        rs4 = A.alloc([2, 4], F32)
        mhalf = A.alloc([4], F32)
        nb4 = A.alloc([4], F32)
        B_qr, B_kr, B_vv, B_vw, B_sgt, B_retb = ([Buf(), Buf()] for _ in range(6))
        B_kqT, B_sT, B_on, B_bn, B_mh = (Buf() for _ in range(5))
        B_SfB = [Buf(), Buf()]
        MEMSET(mhalf, -0.5, [B_mh], eng="gpsimd")
        rot_ctr = [0]

        def rotary(src_bank, Bsrc, rt, Brt, dst, Bdst, i, add_eng="gpsimd"):
            p4 = bank(src_bank).rearrange("p (h a b) -> p h a b", h=4, a=2)
            cosb = rt[:, 0:64].unsqueeze(1).unsqueeze(1).broadcast_to([128, 4, 2, 64])
            sinb = rt[:, 64:128].unsqueeze(1).broadcast_to([128, 4, 64])
            a1 = t1[i].rearrange("p (h a b) -> p h a b", h=4, a=2)
            a2 = t2[i].rearrange("p (h a b) -> p h a b", h=4, a=2)
            TT(a1, p4, cosb, ALU.mult, [Bsrc, Brt], [B_t1[i]])
            STT(a2[:, :, 0, :], p4[:, :, 1, :], -1.0, sinb, ALU.mult, ALU.mult, [Bsrc, Brt], [B_t2[i]])
            TT(a2[:, :, 1, :], p4[:, :, 0, :], sinb, ALU.mult, [Bsrc, Brt], [B_t2[i]])
            TT(dst, t1[i], t2[i], ALU.add, [B_t1[i], B_t2[i]], [Bdst], eng=add_eng)

        def load_rot(n):
            s_ = rot_ctr[0] % 3
            rot_ctr[0] += 1
            DMA("sync", rot[s_], rot_d[n], (), [B_rot[s_]], "rot%d" % s_)
            return s_

        def passA_front(n):
            i = n % 2
            rsl = load_rot(n)
            tk = slice(n * 128, (n + 1) * 128)
            for kc in range(8):
                MM(bank(i), hT[:, kc, tk], wq[:, kc, 512:1024], kc == 0, kc == 7, [B_hT[n], B_wq[kc]], [Bps[i]])
                MM(bank(2 + i), hT[:, kc, tk], wq[:, kc, 1024:1536], kc == 0, kc == 7, [B_hT[n], B_wq[kc]], [Bps[2 + i]])
            rotary(i, Bps[i], rot[rsl], B_rot[rsl], kr[i], B_kr[i], i)
            for h in range(4):
                ACT(vw[i][:, h * 128:(h + 1) * 128], bank(2 + i)[:, h * 128:(h + 1) * 128], AF.Identity, [Bps[2 + i], B_small], [B_vw[i]],
                    scale=posw[:, 1, h:h + 1])

        def passA_back(n):
            i = n % 2
            ub = 4 + i
            for h in range(4):
                MM(bank(ub)[:, h * 128:(h + 1) * 128], kr[i][:, h * 128:(h + 1) * 128], vw[i][:, h * 128:(h + 1) * 128], True, True,
                   [B_kr[i], B_vw[i]], [Bps[ub]])
            src, dst = (SbRun, SbRun2) if n % 2 == 1 else (SbRun2, SbRun)
            Bs_, Bd_ = (B_Sb, B_Sb2) if n % 2 == 1 else (B_Sb2, B_Sb)
            ACT(Sb[:, n, :], src, AF.Identity, [Bs_], [B_Sbn[n]])
            for h in range(4):
                STT(dst[:, h * 128:(h + 1) * 128], src[:, h * 128:(h + 1) * 128], Gd[:, 4 + h:5 + h],
                    bank(ub)[:, h * 128:(h + 1) * 128], ALU.mult, ALU.add, [Bs_, Bps[ub], B_small], [Bd_])

        passA_front(NCH - 1)
        for n in range(NCH - 1, -1, -1):
            if n > 0:
                passA_front(n - 1)
            passA_back(n)
            emit_conv_piece()

        def passB_front_mm(n):
            tk = slice(n * 128, (n + 1) * 128)
            for kc in range(8):
                for j in range(4):
                    MM(bank(j), hT[:, kc, tk], wq[:, kc, j * 512:(j + 1) * 512], kc == 0, kc == 7, [B_hT[n], B_wq_kv, B_wq_qg], [Bps[j]])

        def passB_front_ew(n):
            i = n % 2
            rsl = load_rot(n)
            rotary(0, Bps[0], rot[rsl], B_rot[rsl], qr[i], B_qr[i], 0)
            rotary(1, Bps[1], rot[rsl], B_rot[rsl], kr[i], B_kr[i], 1)
            ACT(vv[i], bank(2), AF.Identity, [Bps[2]], [B_vv[i]])
            for h in range(4):
                ACT(vw[i][:, h * 128:(h + 1) * 128], bank(2)[:, h * 128:(h + 1) * 128], AF.Identity, [Bps[2], B_small], [B_vw[i]],
                    scale=posw[:, 0, h:h + 1])
            ACT(sgt[i], bank(3), AF.Silu, [Bps[3]], [B_sgt[i]])

        def passB_mid1a(n):
            i = n % 2
            pb = bankb(4).rearrange("p (a b) -> p a b", a=8)
            for h in range(4):
                TR(pb[:, h, :], kr[i][:, h * 128:(h + 1) * 128], identB, [B_kr[i], B_small], [Bps[4]])
            for h in range(4):
                TR(pb[:, 4 + h, :], qr[i][:, h * 128:(h + 1) * 128], identB, [B_qr[i], B_small], [Bps[4]])
            ACT(kqT[:, 0:2, :, :].rearrange("p a b c -> p (a b c)"), bankb(4), AF.Identity, [Bps[4]], [B_kqT])
            TT(kqT[:, 2, :, :], pb[:, 4:8, :], WQT[:, 0, :, :], ALU.mult, [Bps[4], B_small], [B_kqT])
            TT(kqT[:, 3, :, :], pb[:, 4:8, :], WQT[:, 1, :, :], ALU.mult, [Bps[4], B_small], [B_kqT])
            srcf = SfRun if n % 2 == 0 else SfRun2
            Bsf = B_Sf if n % 2 == 0 else B_Sf2
            ACT(SfB[n % 2], srcf, AF.Identity, [Bsf], [B_SfB[n % 2]])

        def passB_mid1b(n):
            i = n % 2
            for h in range(4):
                hs = slice(h * 128, (h + 1) * 128)
                MM(bank(7)[:, hs], kr[i][:, hs], vw[i][:, hs], True, True, [B_kr[i], B_vw[i]], [Bps[7]])

        def passB_tail(n):
            i = n % 2
            tk = slice(n * 128, (n + 1) * 128)
            pb2 = bankb(6).rearrange("p (a b) -> p a b", a=8)
            for h in range(4):
                TR(pb2[:, h, :], retb[i][:, h * 128:(h + 1) * 128], identB, [B_retb[i], B_small], [Bps[6]])
            ACT(hT[:, 0:4, tk], pb2[:, 0:4, :], AF.Identity, [Bps[6]], [B_hT[n], B_ret[n]])

        def passB_mid2a(n):
            for h in range(4):
                MM(bank(5)[:, h * 128:(h + 1) * 128], kqT[:, 0, h, :], kqT[:, 1, h, :], True, True, [B_kqT], [Bps[5]])
            TT(sT.rearrange("p a b -> p (a b)"), bank(5), Dm.rearrange("p a b -> p (a b)"), ALU.mult, [Bps[5], B_small], [B_sT])

        def passB_mid2b(n):
            i = n % 2
            for h in range(4):
                hs = slice(h * 128, (h + 1) * 128)
                MM(bank(5)[:, hs], sT[:, h, :], vv[i][:, hs], True, False, [B_sT, B_vv[i]], [Bps[5]])
                MM(bank(5)[:, hs], kqT[:, 2, h, :], SfB[n % 2][:, hs], False, False, [B_kqT, B_SfB[n % 2]], [Bps[5]])
                MM(bank(5)[:, hs], kqT[:, 3, h, :], Sb[:, n, hs], False, True, [B_kqT, B_Sbn[n]], [Bps[5]])
            for h in range(4):
                hs = slice(h * 128, (h + 1) * 128)
                srcf, dstf = (SfRun, SfRun2) if n % 2 == 0 else (SfRun2, SfRun)
                Bsf, Bdf = (B_Sf, B_Sf2) if n % 2 == 0 else (B_Sf2, B_Sf)
                STT(dstf[:, hs], srcf[:, hs], Gd[:, h:h + 1], bank(7)[:, hs], ALU.mult, ALU.add, [Bsf, Bps[7], B_small], [Bdf])
            for h in range(4):
                hs = slice(h * 128, (h + 1) * 128)
                P.op("vector", lambda e, h=h, hs=hs: e.bn_stats(out=bnst[:, h, :], in_=bank(5)[:, hs]), [Bps[5]], [B_bn])
            for h in range(4):
                P.op("vector", lambda e, h=h: e.bn_aggr(out=mv[:, h, :], in_=bnst[:, h, :]), [B_bn], [B_bn])
            ACT(rs4[:, 0, :], mv[:, :, 1], AF.Sqrt, [B_bn], [B_bn], bias=EPS)
            P.op("vector", lambda e: e.reciprocal(out=rs4[:, 1, :], in_=rs4[:, 0, :]), [B_bn], [B_bn])
            STT(nb4, mv[:, :, 0], -1.0, rs4[:, 1, :], ALU.mult, ALU.mult, [B_bn], [B_bn])
            for h in range(4):
                hs = slice(h * 128, (h + 1) * 128)
                ACT(on[:, hs], bank(5)[:, hs], AF.Identity, [Bps[5], B_bn], [B_on], scale=rs4[:, 1, h:h + 1], bias=nb4[:, h:h + 1])
            TT(retb[i], on, sgt[i], ALU.mult, [B_on, B_sgt[i]], [B_retb[i]], eng="gpsimd")

        passB_front_mm(0)
        passB_front_ew(0)
        for n in range(NCH):
            passB_mid1a(n)
            if n + 1 < NCH:
                passB_front_mm(n + 1)
            passB_mid2a(n)
            if n + 1 < NCH:
                passB_front_ew(n + 1)
            passB_mid1b(n)
            if n > 0:
                passB_tail(n - 1)
            passB_mid2b(n)
            emit_conv_piece()
        passB_tail(NCH - 1)
        while conv_pieces:
            emit_conv_piece()
        for bl, key in ((B_w1s, "cv_w1"), (B_w2s, "cv_w2"), (B_wos, "cv_wo")):
            for b_ in bl:
                b_.w = conv_last[key]

        if debug is not None and debug[0] == "ret":
            Bd = Buf()
            DMA("gpsimd", dbg_d, hT[:, 0:4, :].rearrange("p a b -> p (a b)"), B_ret + B_hT, [Bd], "dbg")
            P.op("sync", lambda e: e.nop(), [Bd], ())
            P.finalize(block)
            return nc, A

        P.barrier()
        A.top = mark_persist
        g1b = A.alloc([D], F32)
        g2b = A.alloc([D], F32)
        fgb = A.alloc([D], F32)
        B_g = Buf()
        DMA("sync", g1b, gsc[0].partition_broadcast(128), [B_gsc], [B_g], "c_g")
        DMA("sync", g2b, gsc[1].partition_broadcast(128), [B_gsc], [B_g], "c_g")
        DMA("sync", fgb, fg_d, (), [B_g], "c_g")
        NWS = 3
        wst = [A.alloc([4096], BF16) for _ in range(NWS)]
        B_wst = [Buf() for _ in range(NWS)]
        xs = [A.alloc([D], F32) for _ in range(8)]
        B_xs = [Buf() for _ in range(8)]
        h2 = [A.alloc([D], BF16) for _ in range(4)]
        B_h2 = [Buf() for _ in range(4)]
        h2T = [A.alloc([8, 512], BF16) for _ in range(2)]
        B_h2T = [Buf(), Buf()]
        hidT = hT[:, 4:8, :].rearrange("p a (b c) -> p (a b) c", c=512)
        B_hid = [Buf() for _ in range(32)]
        rl = [A.alloc([512], F32) for _ in range(2)]
        B_rl = [Buf(), Buf()]
        ytmp = rl
        B_yt = B_rl
        ws_ctr = [0]

        def wload(src, Bsrc):
            s_ = ws_ctr[0] % NWS
            ws_ctr[0] += 1
            DMA("sync", wst[s_], src, Bsrc, [B_wst[s_]], "ws%d" % s_)
            return s_

        def mixT(kc):
            return hT[:, kc, :] if kc < 4 else lruT[:, kc - 4, :]

        def op_a(grp):
            g0 = grp * 512
            wo = [wload(wos[u], [B_wos[u]]) for u in range(2)]
            for tt in range(4):
                n = grp * 4 + tt
                xi = (grp % 2) * 4 + tt
                tk = slice(g0 + tt * 128, g0 + (tt + 1) * 128)
                DMA("gpsimd", xs[xi], x_d[tk, :], (), [B_xs[xi]], "x3_%d" % xi)
                for kc in range(8):
                    u = kc // 4
                    wv = wst[wo[u]].rearrange("p (kc n) -> p kc n", kc=4)
                    rd = [B_ret[n], B_hT[n]] if kc < 4 else [B_lruT[kc - 4][grp]]
                    for fh in range(2):
                        MM(bank(4 + fh), mixT(kc)[:, tk], wv[:, kc % 4, fh * 512:(fh + 1) * 512], kc == 0, kc == 7,
                           rd + [B_wst[wo[u]]], [Bps[4 + fh]])
                for fh in range(2):
                    fs = slice(fh * 512, (fh + 1) * 512)
                    TT(ytmp[fh], bank(4 + fh), g1b[:, fs], ALU.mult, [Bps[4 + fh], B_g], [B_yt[fh]])
                    TT(xs[xi][:, fs], ytmp[fh], xs[xi][:, fs], ALU.add, [B_yt[fh], B_xs[xi]], [B_xs[xi]])
                norm_pre(xs[xi], B_xs[xi], h2[tt], B_h2[tt])

        def op_b_tile(grp, tt):
            hb_ = grp % 2
            norm_post(h2[tt], B_h2[tt], 1, 2, lambda kc, tt=tt: h2T[hb_][:, kc, tt * 128:(tt + 1) * 128], B_h2T[hb_], 6)

        def op_b(grp):
            for tt in range(4):
                op_b_tile(grp, tt)

        def mlp1(grp):
            hb2 = grp % 2
            for u in range(8):
                s_ = wload(w1s[u], [B_w1s[u]])
                wv = wst[s_].rearrange("p (kc j) -> p kc j", kc=8)
                for hb_ in range(4):
                    H = u * 4 + hb_
                    bk = (4, 5, 7)[H % 3]
                    for kc in range(8):
                        MM(bank(bk), wv[:, kc, hb_ * 128:(hb_ + 1) * 128], h2T[hb2][:, kc, :], kc == 0, kc == 7,
                           [B_wst[s_], B_h2T[hb2]], [Bps[bk]])
                    ACT(rl[H % 2], bank(bk), AF.Relu, [Bps[bk]], [B_rl[H % 2]])
                    STT(hidT[:, H, :], bank(bk), 0.0, rl[H % 2], ALU.max, ALU.mult, [Bps[bk], B_rl[H % 2]], [B_hid[H]])

        def mlp2(grp, fh, hook=None):
            fs = slice(fh * 512, (fh + 1) * 512)
            for u4 in range(4):
                if hook is not None:
                    hook(u4)
                s_ = wload(w2s[fh * 4 + u4], [B_w2s[fh * 4 + u4]])
                wv = wst[s_].rearrange("p (hb j) -> p hb j", hb=8)
                for tt in range(4):
                    for hb_ in range(8):
                        H = u4 * 8 + hb_
                        MM(bank(tt), hidT[:, H, tt * 128:(tt + 1) * 128], wv[:, hb_, :], H == 0, H == 31,
                           [B_hid[H], B_wst[s_]], [Bps[tt]])
            for tt in range(4):
                xi = (grp % 2) * 4 + tt
                TT(ytmp[tt % 2], bank(tt), g2b[:, fs], ALU.mult, [Bps[tt], B_g], [B_yt[tt % 2]])
                TT(xs[xi][:, fs], ytmp[tt % 2], xs[xi][:, fs], ALU.add, [B_yt[tt % 2], B_xs[xi]], [B_xs[xi]])

        def fin(grp):
            g0 = grp * 512
            for tt in range(4):
                xi = (grp % 2) * 4 + tt
                tk = slice(g0 + tt * 128, g0 + (tt + 1) * 128)
                s_ = stat_ctr[0] % 8
                stat_ctr[0] += 1
                st, Bst = stats[:, s_, :], B_stats[s_]
                ACT(h2[tt], xs[xi], AF.Square, [B_xs[xi]], [B_h2[tt], Bst], accum=st[:, 0:1])
                ACT(st[:, 1:2], st[:, 0:1], AF.Sqrt, [Bst], [Bst], scale=1.0 / D, bias=EPS)
                P.op("vector", lambda e, st=st: e.reciprocal(out=st[:, 2:3], in_=st[:, 1:2]), [Bst], [Bst])
                STT(xs[xi], xs[xi], st[:, 2:3], fgb, ALU.mult, ALU.mult, [B_xs[xi], Bst, B_g], [B_xs[xi]])
                DMA("gpsimd", out_d[tk, :], xs[xi], [B_xs[xi]], [B_xs[xi]], "o3_%d" % xi)

        op_a(0)
        op_b(0)
        for grp in range(8):
            mlp1(grp)
            if grp + 1 < 8:
                op_a(grp + 1)
            mlp2(grp, 0)
            if grp + 1 < 8:
                mlp2(grp, 1, hook=lambda u4, grp=grp: op_b_tile(grp + 1, u4))
            else:
                mlp2(grp, 1)
            fin(grp)
        P.op("sync", lambda e: e.nop(), B_xs, ())
        P.finalize(block)
    return nc, A


def _fm(v):
    return np.ascontiguousarray(np.asarray(v, np.float32).reshape(-1, 128).T)


def host_consts():
    c = np.arange(128, dtype=np.float32)
    posc = np.stack([127 - c, c, c + 1, 128 - c, 255 - c, 127 - c, c, 128 + c], axis=1).astype(np.float32)
    m = c[:, None]
    cc = c[None, :]
    rel = cc - m
    mk = np.stack([np.maximum(rel, 0), (rel >= 0).astype(np.float32), np.maximum(-rel, 0), (rel <= 0).astype(np.float32),
                   np.broadcast_to(cc + 1, (128, 128)), np.broadcast_to(128 - cc, (128, 128))], axis=1).astype(np.float32)
    t = np.arange(T)
    row = (t // 64).astype(np.float32)
    col = (t % 64).astype(np.float32)
    nf = 32
    inv = (np.float32(10000.0) ** (-np.arange(nf, dtype=np.float32) / np.float32(nf))).astype(np.float32)
    ang = np.concatenate([row[:, None] * inv, col[:, None] * inv], axis=-1).astype(np.float32)
    rot = np.concatenate([np.cos(ang), np.sin(ang)], axis=-1).astype(np.float32).reshape(NCH, 128, 128)
    return posc, np.ascontiguousarray(mk.reshape(128, 6 * 128)), np.ascontiguousarray(rot)


def make_in_maps(inputs, cores):
    f = lambda k: np.asarray(inputs[k], np.float32)
    posc, mk, rot = host_consts()
    conv_w = f("conv_w")[0]
    cw = np.ascontiguousarray(conv_w.reshape(4, 4, 128).transpose(2, 1, 0).reshape(128, 16))
    cb = _fm(f("conv_b")[0])
    d4 = lambda a: np.ascontiguousarray(a.reshape(2, 4, 128).transpose(2, 0, 1).reshape(128, 8))

    def bd(w):
        o = np.zeros((128, 2, 4, 128), np.float32)
        for dr in range(2):
            for ct in range(4):
                o[0:64, dr, ct, 0:64] = w[dr, 2 * ct]
                o[64:128, dr, ct, 64:128] = w[dr, 2 * ct + 1]
        return np.ascontiguousarray(o.reshape(128, 1024))

    shared = dict(
        w_ada=np.ascontiguousarray(f("w_ada")[0]), b_ada=np.ascontiguousarray(f("b_ada")[0].reshape(1, -1)),
        w_in=np.ascontiguousarray(f("w_in")[0]), w_out=np.ascontiguousarray(f("w_out")[0]),
        w_mlp1=np.ascontiguousarray(f("w_mlp1")[0]), w_mlp2=np.ascontiguousarray(f("w_mlp2")[0]),
        final_g_rep=np.ascontiguousarray(np.broadcast_to(f("final_g")[None, :], (128, D))),
        wabd=bd(f("lru_wa")[0]), wxbd=bd(f("lru_wx")[0]),
        ident=np.eye(128, dtype=np.float32), mk=mk, rot=rot)
    rdrep = np.ascontiguousarray(np.broadcast_to(f("ret_decay")[0].reshape(1, 8), (128, 8)))
    maps = []
    for b in cores:
        smalls = np.concatenate([
            rdrep, _fm(f("norm1_g")[0]), _fm(f("norm2_g")[0]), _fm(f("c")[b]), _fm(f("c_ctx")),
            cw, cb, d4(f("lru_ba")[0]), d4(f("lru_bx")[0]), d4(f("lru_lambda")[0]), posc], axis=1).astype(np.float32)
        assert smalls.shape == (128, NS)
        m = dict(shared)
        m["x"] = np.ascontiguousarray(f("x")[b])
        m["ctx"] = np.ascontiguousarray(f("ctx")[b])
        m["smalls"] = np.ascontiguousarray(smalls)
        maps.append(m)
    return maps


def kernel(**inputs):
    nc, _ = build_program()
    maps = make_in_maps(inputs, list(range(8)))
    res = run_bass_kernel_spmd(nc, maps, core_ids=list(range(8)))
    out = np.stack([np.asarray(r["out"], np.float32) for r in res.results], axis=0)
    return out
```

```python
import numpy as np
from contextlib import ExitStack
import concourse.bass as bass
import concourse.mybir as mybir
from concourse.bass_utils import run_bass_kernel_spmd

F32 = mybir.dt.float32
BF16 = mybir.dt.bfloat16
AF = mybir.ActivationFunctionType
ALU = mybir.AluOpType

ENGS = ["tensor", "vector", "scalar", "gpsimd", "sync"]
T = 4096
D = 1024
NCH = 32
TC = 256
EPS = 1e-6
KS = 128.0 ** -0.5
RD, N1, N2, CC, CX, CW, CB, BA, BX, LM, PC, NS = 0, 8, 16, 24, 32, 40, 56, 60, 68, 76, 84, 92


class Buf:
    __slots__ = ("name", "w", "r", "excl")

    def __init__(self, name="", excl=False):
        self.name = name
        self.w = None
        self.r = []
        self.excl = excl


class Prog:
    def __init__(self, nc, es):
        self.nc = nc
        self.es = es
        self.insts = []
        self.last_barrier = 0

    def op(self, eng, fn, reads=(), writes=(), dma_key=None, nobarrier=False):
        idx = len(self.insts)
        deps = set()
        xdeps = set()
        for b in reads:
            if b.excl:
                continue
            if b.w is not None:
                deps.add(b.w)
        for b in writes:
            if b.excl:
                continue
            if b.w is not None:
                deps.add(b.w)
            for r in b.r:
                deps.add(r)
        xb = [b for b in list(reads) + list(writes) if b.excl]
        for b in xb:
            if b.w is not None:
                xdeps.add(b.w)
            for r in b.r:
                xdeps.add(r)
        for d in xdeps:
            if self.insts[d]["eng"] != eng or self.insts[d]["dma_key"] is not None:
                deps.add(d)
        for b in reads:
            if not b.excl:
                b.r.append(idx)
        for b in writes:
            if not b.excl:
                b.w = idx
                b.r = []
        for b in xb:
            b.r = [r for r in b.r if self.insts[r]["eng"] != eng]
            if b.w is not None and self.insts[b.w]["eng"] != eng:
                b.r.append(b.w)
            b.w = idx
        deps.discard(idx)
        self.insts.append(dict(eng=eng, fn=fn, deps=deps, dma_key=dma_key, signal=False, nobarrier=nobarrier))
        return idx

    def barrier(self):
        last = {}
        dmas = []
        for i in range(len(self.insts)):
            ins = self.insts[i]
            if not ins.get("nobarrier"):
                last[ins["eng"]] = i
            if ins["dma_key"] is not None and i >= self.last_barrier and not ins.get("nobarrier"):
                dmas.append(i)
        deps = set(last.values()) | set(dmas)
        self.last_barrier = len(self.insts)
        for e in ENGS:
            idx = len(self.insts)
            self.insts.append(dict(eng=e, fn=lambda en: en.nop(), deps=set(deps), dma_key=None, signal=False))

    def finalize(self, block):
        nc, es = self.nc, self.es
        insts = self.insts
        all_dmas = set(i for i, ins in enumerate(insts) if ins["dma_key"] is not None)
        insts.append(dict(eng="sync", fn=lambda en: en.nop(), deps=all_dmas, dma_key=None, signal=False))
        for i, ins in enumerate(insts):
            keep = set()
            for d in ins["deps"]:
                p = insts[d]
                if p["dma_key"] is None and ins["dma_key"] is None and p["eng"] == ins["eng"]:
                    if p["eng"] == "tensor":
                        continue
                keep.add(d)
            ins["deps"] = keep
            for d in keep:
                insts[d]["signal"] = True
        sems, counts = {}, {}
        for e in ENGS:
            sems[e] = es.enter_context(nc.semaphore("s_" + e))
            counts[e] = 0
        for ins in insts:
            if ins["dma_key"] is not None:
                k = ins["dma_key"]
                if k not in sems:
                    sems[k] = es.enter_context(nc.semaphore("d%d" % len(sems)))
                    counts[k] = 0
                counts[k] += 16
                ins["sig"] = (sems[k], 16, counts[k], k)
            elif ins["signal"]:
                e = ins["eng"]
                counts[e] += 1
                ins["sig"] = (sems[e], 1, counts[e], e)
            else:
                ins["sig"] = None
        self.n_sems = len(sems)
        seen = {e: {} for e in ENGS}
        streams = {e: [] for e in ENGS}
        for ins in insts:
            e = ins["eng"]
            waits = {}
            for d in ins["deps"]:
                sem, inc, val, key = insts[d]["sig"]
                if seen[e].get(key, 0) >= val:
                    continue
                if key not in waits or waits[key][1] < val:
                    waits[key] = (sem, val)
            for key, (sem, val) in waits.items():
                seen[e][key] = val
            streams[e].append((list(waits.values()), ins))

        def make(e):
            def body(eng):
                for waits, ins in streams[e]:
                    for sem, val in waits:
                        eng.wait_ge(sem, val)
                    r = ins["fn"](eng)
                    if ins["sig"] is not None:
                        r.then_inc(ins["sig"][0], ins["sig"][1])
            return body

        for e in ENGS:
            if streams[e]:
                getattr(block, e)(make(e))


class Arena:
    def __init__(self, nc, es, nbytes):
        self.t = es.enter_context(nc.sbuf_tensor("arena", [128, nbytes // 4], F32))
        self.top = 0
        self.cap = nbytes
        self.peak = 0

    def alloc(self, shape, dt, parts=128):
        esz = 4 if dt == F32 else 2
        n = int(np.prod(shape))
        nb = (n * esz + 63) // 64 * 64
        off = self.top
        self.top += nb
        self.peak = max(self.peak, self.top)
        assert self.top <= self.cap, ("SBUF arena overflow", self.top, self.cap)
        ap = self.t[0:parts, off // 4:(off + nb) // 4]
        if dt != F32:
            ap = ap.bitcast(dt)
        ap = ap[:, 0:n]
        if len(shape) == 2:
            ap = ap.rearrange("p (a b) -> p a b", a=shape[0], b=shape[1])
        elif len(shape) == 3:
            ap = ap.rearrange("p (a b c) -> p a b c", a=shape[0], b=shape[1], c=shape[2])
        return ap


def build_program(debug=None):
    nc = bass.Bass("TRN2", target_bir_lowering=False)

    def dram(n, s, dt=F32, kind="ExternalInput"):
        return nc.dram_tensor(n, s, dt, kind=kind).ap()

    x_d = dram("x", [T, D])
    ctx_d = dram("ctx", [TC, D])
    smalls_d = dram("smalls", [128, NS])
    wada_d = dram("w_ada", [D, 6 * D])
    bada_d = dram("b_ada", [1, 6 * D])
    win_d = dram("w_in", [D, 3072])
    wout_d = dram("w_out", [D, D])
    w1_d = dram("w_mlp1", [D, 4096])
    w2_d = dram("w_mlp2", [4096, D])
    fg_d = dram("final_g_rep", [128, D])
    wabd_d = dram("wabd", [128, 1024])
    wxbd_d = dram("wxbd", [128, 1024])
    ident_d = dram("ident", [128, 128])
    mk_d = dram("mk", [128, 6 * 128])
    rot_d = dram("rot", [NCH, 128, 128])
    out_d = dram("out", [T, D], kind="ExternalOutput")
    w1s = dram("w1s", [8, 128, 4096], BF16, kind="Internal")
    w2s = dram("w2s", [8, 128, 4096], BF16, kind="Internal")
    wos = dram("wos", [2, 128, 4096], BF16, kind="Internal")
    gsc = dram("gsc", [2, D], F32, kind="Internal")
    wqs = dram("wqs", [128, 8, 2048], BF16, kind="Internal")
    dbg_d = None
    if debug is not None:
        dbg_d = dram("dbg", list(debug[1]), F32, kind="ExternalOutput")

    es = ExitStack()
    with es:
        A = Arena(nc, es, 212480)
        psAll = es.enter_context(nc.psum_tensor("psAll", [128, 4096], F32))
        block = es.enter_context(nc.Block())
        P = Prog(nc, es)
        Bps = [Buf("ps%d" % i, excl=True) for i in range(8)]

        def bank(b):
            return psAll[:, b * 512:(b + 1) * 512]

        def bankb(b):
            return psAll[:, b * 512:(b + 1) * 512].bitcast(BF16)

        def MM(out, lhsT, rhs, start, stop, r, w):
            P.op("tensor", lambda e: e.matmul(out, lhsT=lhsT, rhs=rhs, start=start, stop=stop), r, w)

        def TR(out, in_, ident, r, w):
            P.op("tensor", lambda e: e.transpose(out=out, in_=in_, identity=ident), r, w)

        def ACT(out, in_, func, r, w, scale=None, bias=None, accum=None):
            kw = {}
            if scale is not None:
                kw["scale"] = scale
            if bias is not None:
                kw["bias"] = bias
            if accum is not None:
                kw["accum_out"] = accum
            P.op("scalar", lambda e: e.activation(out=out, in_=in_, func=func, **kw), r, w)

        def TT(out, in0, in1, op, r, w, eng="vector"):
            P.op(eng, lambda e: e.tensor_tensor(out=out, in0=in0, in1=in1, op=op), r, w)

        def TS(out, in0, s1, s2, op0, op1, r, w, eng="vector"):
            if op1 is None:
                P.op(eng, lambda e: e.tensor_scalar(out=out, in0=in0, scalar1=s1, scalar2=None, op0=op0), r, w)
            else:
                P.op(eng, lambda e: e.tensor_scalar(out=out, in0=in0, scalar1=s1, scalar2=s2, op0=op0, op1=op1), r, w)

        def STT(out, in0, scalar, in1, op0, op1, r, w):
            P.op("vector", lambda e: e.scalar_tensor_tensor(out=out, in0=in0, scalar=scalar, in1=in1, op0=op0, op1=op1), r, w)

        def CP(out, in_, r, w, eng="vector"):
            P.op(eng, lambda e: e.tensor_copy(out=out, in_=in_), r, w)

        def MEMSET(ap, val, w, eng="vector"):
            P.op(eng, lambda e: e.memset(ap, val), (), w)

        def DMA(eng, out, in_, r, w, key, nobarrier=False):
            P.op(eng, lambda e: e.dma_start(out=out, in_=in_), r, w, dma_key=key, nobarrier=nobarrier)

        B_hT = [Buf("hT%d" % i) for i in range(NCH)]
        B_lruT = [[Buf("lruT") for _ in range(8)] for _ in range(4)]
        B_ret = [Buf("ret%d" % i) for i in range(NCH)]
        smalls = A.alloc([NS], F32)
        identF = A.alloc([128], F32)
        identB = A.alloc([128], BF16)
        lg = A.alloc([8], F32)
        cch = A.alloc([8], F32)
        cch2 = A.alloc([8], F32)
        posw = A.alloc([8, 4], F32)
        Gd = A.alloc([8], F32)
        Dm = A.alloc([4, 128], BF16)
        WQT = A.alloc([2, 4, 128], F32)
        scl = A.alloc([3, 8], F32)
        fm = A.alloc([6, 8], F32)
        lru_s0 = A.alloc([2, 4], F32)
        SfRun = A.alloc([512], F32)
        SbRun = A.alloc([512], F32)
        zcol = A.alloc([1], F32)
        stats = A.alloc([8, 4], F32)
        B_small = Buf("small")
        B_stats = [Buf("st%d" % i) for i in range(8)]
        B_Sf, B_Sb = Buf("Sf"), Buf("Sb")
        B_s0 = Buf("s0")
        hT_off = A.top
        hT = A.alloc([8, T], BF16)
        lruT_off = A.top
        lruT = A.alloc([4, T], BF16)
        mark_persist = A.top
        hcT = A.alloc([8, TC], BF16)
        wabd = A.alloc([2, 4, 128], BF16)
        wxbd = A.alloc([2, 4, 128], BF16)
        B_hcT, B_wbd = Buf("hcT"), Buf("wbd")
        mark_A = A.top

        B_w1s = [Buf("w1s%d" % u) for u in range(8)]
        B_w2s = [Buf("w2s%d" % u) for u in range(8)]
        B_wos = [Buf("wos%d" % u) for u in range(2)]

        conv_pieces = []
        conv_last = {}

        def build_weight_conversion():
            w1v = w1_d.rearrange("(kc p) (u j) -> u p kc j", p=128, j=512)
            for u in range(8):
                dst = w1s[u].rearrange("p (kc j) -> p kc j", kc=8)
                for hh in range(2):
                    conv_pieces.append((dst[:, hh * 4:(hh + 1) * 4, :], w1v[u][:, hh * 4:(hh + 1) * 4, :], B_w1s[u], "cv_w1"))
            w2v = w2_d.rearrange("(u hb p) (fh j) -> fh u p hb j", p=128, hb=8, j=512)
            for fh in range(2):
                for u4 in range(4):
                    dst = w2s[fh * 4 + u4].rearrange("p (hb j) -> p hb j", hb=8)
                    for hh in range(2):
                        conv_pieces.append((dst[:, hh * 4:(hh + 1) * 4, :], w2v[fh, u4][:, hh * 4:(hh + 1) * 4, :],
                                            B_w2s[fh * 4 + u4], "cv_w2"))
            wov = wout_d.rearrange("(u kc p) n -> u p kc n", p=128, kc=4)
            for u in range(2):
                dst = wos[u].rearrange("p (kc n) -> p kc n", kc=4)
                for hh in range(2):
                    conv_pieces.append((dst[:, hh * 2:(hh + 1) * 2, :], wov[u][:, hh * 2:(hh + 1) * 2, :], B_wos[u], "cv_wo"))

        def emit_conv_piece():
            if conv_pieces:
                dst, src, Bd, key = conv_pieces.pop(0)
                P.op("gpsimd", lambda e, dst=dst, src=src: e.dma_start(out=dst, in_=src), (), (), dma_key=key, nobarrier=True)
                conv_last[key] = len(P.insts) - 1

        DMA("sync", smalls, smalls_d, (), [B_small], "c_small")
        DMA("sync", identF, ident_d, (), [B_small], "c_small")
        DMA("gpsimd", identB, ident_d, (), [B_small], "c_identb")
        DMA("gpsimd", wabd.rearrange("p a b c -> p (a b c)"), wabd_d, (), [B_wbd], "c_wbd")
        DMA("gpsimd", wxbd.rearrange("p a b c -> p (a b c)"), wxbd_d, (), [B_wbd], "c_wbd")

        A.top = lruT_off
        wada_sl = [A.alloc([8, 1024], BF16) for _ in range(2)]
        assert A.top <= mark_persist
        A.top = mark_A
        wxg = A.alloc([8, 1024], BF16)
        B_wxg = Buf()
        mark_A2 = A.top
        NX1 = 3
        xs1 = [A.alloc([D], F32) for _ in range(NX1)]
        xn1 = [A.alloc([D], BF16) for _ in range(3)]
        junk1 = A.alloc([D], BF16)
        B_xs1 = [Buf() for _ in range(NX1)]
        B_xn1 = [Buf() for _ in range(3)]
        mk = A.alloc([6, 128], F32)
        tmpA = A.alloc([128], F32)
        tmpB = A.alloc([128], F32)
        tmp8 = A.alloc([4, 8], F32)
        onesF = A.alloc([128], F32)
        ones1 = A.alloc([128], BF16, parts=1)
        bada_sl = [A.alloc([1024], BF16, parts=1) for _ in range(2)]
        silc = A.alloc([2, 8], F32)
        crep = A.alloc([2, 8, 128], BF16)
        mblk = [A.alloc([512], F32) for _ in range(4)]
        B_mblk = [Buf() for _ in range(4)]
        wkv = A.alloc([8, 1024], BF16)
        xct = [A.alloc([D], F32) for _ in range(2)]
        xcn = [A.alloc([D], BF16) for _ in range(2)]
        kct = A.alloc([2, 512], BF16)
        vcw = A.alloc([2, 2, 512], BF16)
        B_mk, B_tA, B_tB, B_t8, B_ones = Buf(), Buf(), Buf(), Buf(), Buf()
        B_badab = [Buf(), Buf()]
        B_silc, B_crep, B_wkv = Buf(), Buf(), Buf()
        B_wada = [Buf(), Buf()]
        B_xct, B_xcn = [Buf(), Buf()], [Buf(), Buf()]
        B_kct, B_vcw = Buf(), Buf()
        B_fm = Buf("fm")
        B_fm1 = Buf("fm1")

        DMA("sync", mk.rearrange("p a b -> p (a b)"), mk_d, (), [B_mk], "c_mk")
        wadav = wada_d.rearrange("(kc p) n -> p kc n", p=128)
        for q in range(2):
            DMA("gpsimd", wada_sl[q], wadav[:, :, q * 1024:(q + 1) * 1024], (), [B_wada[q]], "c_wada%d" % q)
            DMA("gpsimd", bada_sl[q], bada_d[:, q * 1024:(q + 1) * 1024], (), [B_badab[q]], "c_bada%d" % q)
        winv = win_d.rearrange("(kc p) n -> p kc n", p=128)
        DMA("gpsimd", wkv, winv[:, :, 512:1536], (), [B_wkv], "c_wkv")
        DMA("gpsimd", wxg, winv[:, :, 2048:3072], (), [B_wxg], "c_wxg")
        for t in range(2):
            DMA("sync", xct[t], ctx_d[t * 128:(t + 1) * 128, :], (), [B_xct[t]], "c_ctx%d" % t)

        MEMSET(onesF, 1.0, [B_ones])
        MEMSET(ones1, 1.0, [B_ones])
        MEMSET(zcol, 0.0, [B_small])
        ACT(tmp8[:, 0, :], smalls[:, RD:RD + 8], AF.Sigmoid, [B_small], [B_t8])
        ACT(tmp8[:, 1, :], smalls[:, LM:LM + 8], AF.Sigmoid, [B_small], [B_t8])
        ACT(tmp8[:, 2, :], smalls[:, CC:CC + 8], AF.Silu, [B_small], [B_t8])
        ACT(tmp8[:, 3, :], smalls[:, CX:CX + 8], AF.Silu, [B_small], [B_t8])
        ACT(lg, tmp8[:, 0, :], AF.Ln, [B_t8], [B_small])
        ACT(cch, tmp8[:, 1, :], AF.Ln, [B_t8], [B_small])
        TS(cch2, cch, 16.0, None, ALU.mult, None, [B_small], [B_small])
        TS(cch, cch, 8.0, None, ALU.mult, None, [B_small], [B_small])
        for j in range(8):
            dr = 0 if j in (0, 2, 4, 5) else 1
            TS(posw[:, j, :], lg[:, dr * 4:(dr + 1) * 4], smalls[:, PC + j:PC + j + 1], None, ALU.mult, None,
               [B_small], [B_small])
        ACT(posw.rearrange("p a b -> p (a b)"), posw.rearrange("p a b -> p (a b)"), AF.Exp, [B_small], [B_small])
        ACT(Gd, lg, AF.Exp, [B_small], [B_small], scale=128.0)
        TS(posw[:, 0:2, :], posw[:, 0:2, :], KS, None, ALU.mult, None, [B_small], [B_small])
        TS(posw[:, 4:8, :], posw[:, 4:8, :], KS, None, ALU.mult, None, [B_small], [B_small])
        for h in range(4):
            ACT(tmpA, mk[:, 0, :], AF.Exp, [B_mk, B_small], [B_tA], scale=lg[:, h:h + 1])
            TT(tmpA, tmpA, mk[:, 1, :], ALU.mult, [B_mk, B_tA], [B_tA])
            ACT(tmpB, mk[:, 2, :], AF.Exp, [B_mk, B_small], [B_tB], scale=lg[:, 4 + h:5 + h])
            TT(tmpB, tmpB, mk[:, 3, :], ALU.mult, [B_mk, B_tB], [B_tB])
            TT(tmpA, tmpA, tmpB, ALU.add, [B_tA, B_tB], [B_tA])
            TS(Dm[:, h, :], tmpA, KS, None, ALU.mult, None, [B_tA], [B_small])
            ACT(WQT[:, 0, h, :], mk[:, 4, :], AF.Exp, [B_mk, B_small], [B_small], scale=lg[:, h:h + 1])
            ACT(WQT[:, 1, h, :], mk[:, 5, :], AF.Exp, [B_mk, B_small], [B_small], scale=lg[:, 4 + h:5 + h])
        for v in range(2):
            for kc in range(8):
                TS(crep[:, v, kc, :], onesF, tmp8[:, 2 + v, kc:kc + 1], None, ALU.mult, None, [B_ones, B_t8], [B_crep])
        stat_ctr = [0]

        def norm_pre(src, Bsrc, xn, Bxn):
            s = stat_ctr[0] % 8
            stat_ctr[0] += 1
            st, Bst = stats[:, s, :], B_stats[s]
            ACT(xn, src, AF.Square, [Bsrc], [Bxn, Bst], accum=st[:, 0:1])
            ACT(st[:, 1:2], st[:, 0:1], AF.Sqrt, [Bst], [Bst], scale=1.0 / D, bias=EPS)
            P.op("vector", lambda e: e.reciprocal(out=st[:, 2:3], in_=st[:, 1:2]), [Bst], [Bst])
            ACT(xn, src, AF.Identity, [Bsrc, Bst], [Bxn], scale=st[:, 2:3])

        def norm_post(xn, Bxn, sidx, bidx, dst_fn, Bdst, tbank):
            pb = bankb(tbank).rearrange("p (a b) -> p a b", a=8)
            for kc in range(8):
                TR(pb[:, kc, :], xn[:, kc * 128:(kc + 1) * 128], identB, [Bxn, B_small], [Bps[tbank]])
            for kc in range(8):
                if False:
                    ACT(dst_fn(kc), pb[:, kc, :], AF.Identity, [Bps[tbank], B_fm], [Bdst],
                        scale=scl[:, sidx, kc:kc + 1], bias=fm[:, bidx, kc:kc + 1])
                else:
                    TS(dst_fn(kc), pb[:, kc, :], scl[:, sidx, kc:kc + 1], fm[:, bidx, kc:kc + 1], ALU.mult, ALU.add,
                       [Bps[tbank], B_fm1], [Bdst])

        def norm_T(src, Bsrc, xn, Bxn, sidx, bidx, dst_fn, Bdst, tbank):
            norm_pre(src, Bsrc, xn, Bxn)
            norm_post(xn, Bxn, sidx, bidx, dst_fn, Bdst, tbank)

        p1_st = {}

        def p1_stats(tt):
            s_ = tt % NX1
            DMA("sync", xs1[s_], x_d[tt * 128:(tt + 1) * 128, :], (), [B_xs1[s_]], "x1_%d" % s_)
            k = stat_ctr[0] % 8
            stat_ctr[0] += 1
            st, Bst = stats[:, k, :], B_stats[k]
            ACT(junk1, xs1[s_], AF.Square, [B_xs1[s_]], [Bst], accum=st[:, 0:1])
            ACT(st[:, 1:2], st[:, 0:1], AF.Sqrt, [Bst], [Bst], scale=1.0 / D, bias=EPS)
            P.op("vector", lambda e, st=st: e.reciprocal(out=st[:, 2:3], in_=st[:, 1:2]), [Bst], [Bst])
            p1_st[tt] = (st, Bst)

        def p1_apply(tt):
            s_ = tt % NX1
            st, Bst = p1_st[tt]
            ACT(xn1[tt % 3], xs1[s_], AF.Identity, [B_xs1[s_], Bst], [B_xn1[tt % 3]], scale=st[:, 2:3])

        def p1_post(tt):
            norm_post(xn1[tt % 3], B_xn1[tt % 3], 0, 0, lambda kc, tt=tt: hT[:, kc, tt * 128:(tt + 1) * 128], B_hT[tt], tt % 2)

        def emit_phase1():
            for tt in range(NCH + 2):
                if tt < NCH:
                    p1_stats(tt)
                if 1 <= tt <= NCH:
                    p1_apply(tt - 1)
                if tt >= 2:
                    p1_post(tt - 2)


        B_gsc = Buf("gsc")
        mb_ctr = [0]
        vi_of = {0: 0, 1: 1, 3: 2, 4: 3}
        for q in range(6):
            sl = q % 2
            if q >= 2:
                DMA("gpsimd", wada_sl[sl], wadav[:, :, q * 1024:(q + 1) * 1024], (), [B_wada[sl]], "c_wada%d" % sl)
                DMA("gpsimd", bada_sl[sl], bada_d[:, q * 1024:(q + 1) * 1024], (), [B_badab[sl]], "c_bada%d" % sl)
            for j in range(2):
                J = 2 * q + j
                vo, half = J // 2, J % 2
                for v in ((0, 1) if vo < 2 else (0,)):
                    ms = mb_ctr[0] % 4
                    mb_ctr[0] += 1
                    bk = ms
                    for kc in range(8):
                        MM(bank(bk), crep[:, v, kc, :], wada_sl[sl][:, kc, j * 512:(j + 1) * 512], kc == 0, False,
                           [B_crep, B_wada[sl]], [Bps[bk]])
                    MM(bank(bk), ones1[0:1, :], bada_sl[sl][0:1, j * 512:(j + 1) * 512], False, True,
                       [B_ones, B_badab[sl]], [Bps[bk]])
                    ACT(mblk[ms], bank(bk), AF.Identity, [Bps[bk]], [B_mblk[ms]])
                    if vo in (2, 5):
                        gi = 0 if vo == 2 else 1
                        DMA("sync", gsc[gi:gi + 1, half * 512:(half + 1) * 512], mblk[ms][0:1, :], [B_mblk[ms]], [B_gsc], "c_gsc")
                    else:
                        vi = vi_of[vo] if v == 0 else 4 + vo
                        tb = 4 + ms
                        for k4 in range(4):
                            TR(bank(tb)[:, k4 * 128:(k4 + 1) * 128], mblk[ms][:, k4 * 128:(k4 + 1) * 128], identF,
                               [B_mblk[ms], B_small], [Bps[tb]])
                        CP(fm[:, vi, half * 4:(half + 1) * 4], bank(tb).rearrange("p (a b) -> p a b", a=4)[:, :, 0], [Bps[tb]], [B_fm])
            if q == 1:
                STT(scl[:, 0, :], fm[:, 1, :], 1.0, smalls[:, N1:N1 + 8], ALU.add, ALU.mult, [B_fm, B_small], [B_fm1])
                STT(scl[:, 2, :], fm[:, 5, :], 1.0, smalls[:, N1:N1 + 8], ALU.add, ALU.mult, [B_fm, B_small], [B_fm1])
                emit_phase1()
        STT(scl[:, 1, :], fm[:, 3, :], 1.0, smalls[:, N2:N2 + 8], ALU.add, ALU.mult, [B_fm, B_small], [B_fm1])

        for t in range(2):
            norm_T(xct[t], B_xct[t], xcn[t], B_xcn[t], 2, 4, lambda kc, t=t: hcT[:, kc, t * 128:(t + 1) * 128], B_hcT, 6 + t)
        for t in range(2):
            for kc in range(8):
                MM(bank(0), hcT[:, kc, t * 128:(t + 1) * 128], wkv[:, kc, 0:512], kc == 0, kc == 7, [B_hcT, B_wkv], [Bps[0]])
            for kc in range(8):
                MM(bank(1), hcT[:, kc, t * 128:(t + 1) * 128], wkv[:, kc, 512:1024], kc == 0, kc == 7, [B_hcT, B_wkv], [Bps[1]])
            ACT(kct[:, t, :], bank(0), AF.Identity, [Bps[0]], [B_kct])
            for dr in range(2):
                for h in range(4):
                    j = 4 + 2 * dr + t
                    sc = posw[:, j, h:h + 1]
                    if (dr + h) % 2 == 0:
                        ACT(vcw[:, t, dr, h * 128:(h + 1) * 128], bank(1)[:, h * 128:(h + 1) * 128], AF.Identity,
                            [Bps[1], B_small], [B_vcw], scale=sc)
                    else:
                        TS(vcw[:, t, dr, h * 128:(h + 1) * 128], bank(1)[:, h * 128:(h + 1) * 128], sc, None, ALU.mult, None,
                           [Bps[1], B_small], [B_vcw])
        for dr in range(2):
            bk = 2 + dr
            for h in range(4):
                for t in range(2):
                    MM(bank(bk)[:, h * 128:(h + 1) * 128], kct[:, t, h * 128:(h + 1) * 128], vcw[:, t, dr, h * 128:(h + 1) * 128],
                       t == 0, t == 1, [B_kct, B_vcw], [Bps[bk]])
            if dr == 0:
                ACT(SfRun, bank(bk), AF.Identity, [Bps[bk]], [B_Sf])
            else:
                CP(SbRun, bank(bk), [Bps[bk]], [B_Sb])

        if debug is not None and debug[0] == "ctx":
            Bd = Buf()
            DMA("sync", dbg_d[:, 0:512], SfRun, [B_Sf], [Bd], "dbg")
            DMA("sync", dbg_d[:, 512:1024], SbRun, [B_Sb], [Bd], "dbg")
            DMA("gpsimd", dbg_d[:, 1024:3072], hcT.rearrange("p a b -> p (a b)"), [B_hcT], [Bd], "dbg")
            DMA("sync", dbg_d[:, 3072:3072 + 48], fm.rearrange("p a b -> p (a b)"), [B_fm], [Bd], "dbg")
            DMA("sync", dbg_d[:, 3120:3120 + 24], scl.rearrange("p a b -> p (a b)"), [B_fm], [Bd], "dbg")
            P.op("sync", lambda e: e.nop(), [Bd], ())
            P.finalize(block)
            return nc, A

        if debug is not None and debug[0] == "hT":
            Bd = Buf()
            DMA("gpsimd", dbg_d, hT.rearrange("p a b -> p (a b)"), B_hT, [Bd], "dbg")
            P.op("sync", lambda e: e.nop(), [Bd], ())
            P.finalize(block)
            return nc, A

        P.barrier()
        A.top = mark_A2
        B_wqs = Buf("wqs")
        DMA("gpsimd", wqs, winv[:, :, 0:2048], (), [B_wqs], "cv_wq", nobarrier=True)
        xc = A.alloc([T], F32)
        xcb = A.alloc([T], BF16)
        acc = A.alloc([T], F32)
        QS = 512
        Rb = [[A.alloc([QS], F32) for _ in range(2)] for _ in range(2)]
        Ib = [[A.alloc([QS], F32) for _ in range(2)] for _ in range(2)]
        Qb = [[A.alloc([QS], F32) for _ in range(2)] for _ in range(2)]
        Hb = [[A.alloc([QS], F32) for _ in range(2)] for _ in range(2)]
        gl = [A.alloc([512], F32) for _ in range(2)]
        carry = A.alloc([2], F32)
        B_xc, B_carry = Buf(), [Buf(), Buf()]
        B_xcbq = [Buf() for _ in range(8)]
        B_acc = [Buf() for _ in range(8)]
        B_R = [[Buf(), Buf()], [Buf(), Buf()]]
        B_I = [[Buf(), Buf()], [Buf(), Buf()]]
        B_Q = [[Buf(), Buf()], [Buf(), Buf()]]
        B_H = [[Buf(), Buf()], [Buf(), Buf()]]
        B_gl = [Buf(), Buf()]

        def lru_stage(Tn, srcT, B_src_of, is_ctx, extra_w=()):
            nsl = max(1, Tn // 512)
            W = min(512, Tn)
            NQ = Tn // W
            def emit_xr(ct_, s_):
                for kc in range(8):
                    MM(bank(s_)[:, 0:W], wxg[:, kc, ct_ * 128:(ct_ + 1) * 128], srcT[:, kc, s_ * 512:s_ * 512 + W], kc == 0, kc == 7,
                       [B_wxg] + B_src_of(s_), [Bps[s_]])

            xr_done = set()
            for ct in range(4):
                for s_ in range(nsl):
                    if (ct, s_) not in xr_done:
                        emit_xr(ct, s_)
                pr = list(Bps[0:nsl])
                cw = lambda j: smalls[:, CW + ct * 4 + j:CW + ct * 4 + j + 1]
                cb = smalls[:, CB + ct:CB + ct + 1]
                xw = list(extra_w) if ct == 0 else []
                TS(xc[:, 1:Tn], psAll[:, 0:Tn - 1], cw(0), cb, ALU.mult, ALU.add, pr + [B_small], [B_xc] + xw)
                ACT(xc[:, 0:1], zcol, AF.Identity, [B_small], [B_xc], scale=1.0, bias=cb)
                STT(xc[:, 0:Tn], psAll[:, 0:Tn], cw(1), xc[:, 0:Tn], ALU.mult, ALU.add, pr + [B_small, B_xc], [B_xc])
                STT(xc[:, 0:Tn - 1], psAll[:, 1:Tn], cw(2), xc[:, 0:Tn - 1], ALU.mult, ALU.add, pr + [B_small, B_xc], [B_xc])
                STT(xc[:, 0:Tn - 2], psAll[:, 2:Tn], cw(3), xc[:, 0:Tn - 2], ALU.mult, ALU.add, pr + [B_small, B_xc], [B_xc])
                order = []
                for j in range(NQ):
                    for q_ in (j, NQ - 1 - j):
                        if q_ not in order:
                            order.append(q_)
                for oi, q_ in enumerate(order):
                    if oi % 4 == 3:
                        ACT(xcb[:, q_ * W:(q_ + 1) * W], xc[:, q_ * W:(q_ + 1) * W], AF.Identity, [B_xc], [B_xcbq[q_]])
                    else:
                        CP(xcb[:, q_ * W:(q_ + 1) * W], xc[:, q_ * W:(q_ + 1) * W], [B_xc], [B_xcbq[q_]])
                for j in range(NQ):
                    st_ = j % 2
                    qd = (j, NQ - 1 - j)
                    for dr in range(2):
                        t0 = qd[dr] * W
                        for (k2, wsrc, dst, Bd, bcol) in ((0, wabd, Rb, B_R, BA), (1, wxbd, Ib, B_I, BX)):
                            bk = st_ * 4 + dr * 2 + k2
                            MM(bank(bk)[:, 0:W], wsrc[:, dr, ct, :], xcb[:, t0:t0 + W], True, True, [B_wbd, B_xcbq[qd[dr]]], [Bps[bk]])
                            ACT(dst[st_][dr][:, 0:W], bank(bk)[:, 0:W], AF.Sigmoid, [Bps[bk], B_small], [Bd[st_][dr]],
                                bias=smalls[:, bcol + dr * 4 + ct:bcol + dr * 4 + ct + 1])
                    for dr in range(2):
                        R, I, Q = Rb[st_][dr][:, 0:W], Ib[st_][dr][:, 0:W], Qb[st_][dr][:, 0:W]
                        ccol = cch[:, dr * 4 + ct:dr * 4 + ct + 1]
                        ACT(R, R, AF.Exp, [B_R[st_][dr], B_small], [B_R[st_][dr]], scale=ccol)
                    for dr in range(2):
                        R, I, Q = Rb[st_][dr][:, 0:W], Ib[st_][dr][:, 0:W], Qb[st_][dr][:, 0:W]
                        t0 = qd[dr] * W
                        TT(Q, R, R, ALU.mult, [B_R[st_][dr]], [B_Q[st_][dr]], eng="gpsimd")
                        TT(I, I, xc[:, t0:t0 + W], ALU.mult, [B_I[st_][dr], B_xc], [B_I[st_][dr]], eng="gpsimd")
                    for dr in range(2):
                        Q = Qb[st_][dr][:, 0:W]
                        ACT(Q, Q, AF.Ln, [B_Q[st_][dr]], [B_Q[st_][dr]], scale=-1.0, bias=1.0)
                    for dr in range(2):
                        Q = Qb[st_][dr][:, 0:W]
                        ACT(Q, Q, AF.Exp, [B_Q[st_][dr]], [B_Q[st_][dr]], scale=0.5)
                    for dr in range(2):
                        R, I, Q, H = Rb[st_][dr][:, 0:W], Ib[st_][dr][:, 0:W], Qb[st_][dr][:, 0:W], Hb[st_][dr][:, 0:W]
                        t0 = qd[dr] * W
                        TT(I, I, Q, ALU.mult, [B_I[st_][dr], B_Q[st_][dr]], [B_I[st_][dr]], eng="gpsimd")
                        if j == 0:
                            init = zcol if is_ctx else lru_s0[:, dr, ct:ct + 1]
                            Binit = B_small if is_ctx else B_s0
                        else:
                            init = carry[:, dr:dr + 1]
                            Binit = B_carry[dr]
                        if dr == 0:
                            P.op("vector", lambda e, H=H, R=R, I=I, init=init: e.tensor_tensor_scan(
                                out=H, data0=R, data1=I, initial=init, op0=ALU.mult, op1=ALU.add),
                                [B_R[st_][dr], B_I[st_][dr], Binit], [B_H[st_][dr]])
                            CP(carry[:, 0:1], H[:, W - 1:W], [B_H[st_][dr]], [B_carry[0]])
                        else:
                            P.op("vector", lambda e, H=H, R=R, I=I, init=init: e.tensor_tensor_scan(
                                out=H[:, ::-1], data0=R[:, ::-1], data1=I[:, ::-1], initial=init, op0=ALU.mult, op1=ALU.add),
                                [B_R[st_][dr], B_I[st_][dr], Binit], [B_H[st_][dr]])
                            CP(carry[:, 1:2], H[:, 0:1], [B_H[st_][dr]], [B_carry[1]])
                        if not is_ctx:
                            q_ = qd[dr]
                            other_step = NQ - 1 - j
                            first = j < other_step or (j == other_step and dr == 0)
                            if first:
                                CP(acc[:, t0:t0 + W], H, [B_H[st_][dr]], [B_acc[q_]])
                            else:
                                TT(acc[:, t0:t0 + W], acc[:, t0:t0 + W], H, ALU.add, [B_H[st_][dr], B_acc[q_]], [B_acc[q_]])
                        elif j == NQ - 1:
                            if dr == 0:
                                CP(lru_s0[:, 0, ct:ct + 1], H[:, W - 1:W], [B_H[st_][dr]], [B_s0])
                            else:
                                CP(lru_s0[:, 1, ct:ct + 1], H[:, 0:1], [B_H[st_][dr]], [B_s0])
                if not is_ctx:
                    for sg in range(8):
                        bk = 6 + (sg % 2)
                        for kc in range(8):
                            MM(bank(bk), wxg[:, kc, 512 + ct * 128:512 + (ct + 1) * 128], srcT[:, kc, sg * 512:(sg + 1) * 512],
                               kc == 0, kc == 7, [B_wxg] + B_src_of(sg), [Bps[bk]])
                        g = gl[sg % 2]
                        ACT(g, bank(bk), AF.Gelu, [Bps[bk]], [B_gl[sg % 2]])
                        TT(lruT[:, ct, sg * 512:(sg + 1) * 512], acc[:, sg * 512:(sg + 1) * 512], g, ALU.mult,
                           [B_acc[sg], B_gl[sg % 2]], [B_lruT[ct][sg]])
                        if ct + 1 < 4 and sg < 6:
                            emit_xr(ct + 1, sg)
                            xr_done.add((ct + 1, sg))

        B_xcC = [Buf() for _ in range(4)]
        B_xcbC = [Buf() for _ in range(4)]

        def lru_ctx():
            W = TC
            for ct in range(4):
                for kc in range(8):
                    MM(bank(ct)[:, 0:W], wxg[:, kc, ct * 128:(ct + 1) * 128], hcT[:, kc, 0:W], kc == 0, kc == 7,
                       [B_wxg, B_hcT], [Bps[ct]])
            for ct in range(4):
                ps = bank(ct)
                xcc = xc[:, ct * W:(ct + 1) * W]
                cw = lambda j, ct=ct: smalls[:, CW + ct * 4 + j:CW + ct * 4 + j + 1]
                cb = smalls[:, CB + ct:CB + ct + 1]
                TS(xcc[:, 1:W], ps[:, 0:W - 1], cw(0), cb, ALU.mult, ALU.add, [Bps[ct], B_small], [B_xcC[ct]])
                ACT(xcc[:, 0:1], zcol, AF.Identity, [B_small], [B_xcC[ct]], scale=1.0, bias=cb)
                STT(xcc[:, 0:W], ps[:, 0:W], cw(1), xcc[:, 0:W], ALU.mult, ALU.add, [Bps[ct], B_small, B_xcC[ct]], [B_xcC[ct]])
                STT(xcc[:, 0:W - 1], ps[:, 1:W], cw(2), xcc[:, 0:W - 1], ALU.mult, ALU.add, [Bps[ct], B_small, B_xcC[ct]], [B_xcC[ct]])
                STT(xcc[:, 0:W - 2], ps[:, 2:W], cw(3), xcc[:, 0:W - 2], ALU.mult, ALU.add, [Bps[ct], B_small, B_xcC[ct]], [B_xcC[ct]])
                CP(xcb[:, ct * W:(ct + 1) * W], xcc, [B_xcC[ct]], [B_xcbC[ct]])
            for pair in range(2):
                cts = (2 * pair, 2 * pair + 1)
                for ct in cts:
                    st_ = ct % 2
                    for dr in range(2):
                        for (k2, wsrc, dst, Bd, bcol) in ((0, wabd, Rb, B_R, BA), (1, wxbd, Ib, B_I, BX)):
                            bk = 4 + dr * 2 + k2
                            MM(bank(bk)[:, 0:W], wsrc[:, dr, ct, :], xcb[:, ct * W:(ct + 1) * W], True, True, [B_wbd, B_xcbC[ct]], [Bps[bk]])
                            ACT(dst[st_][dr][:, 0:W], bank(bk)[:, 0:W], AF.Sigmoid, [Bps[bk], B_small], [Bd[st_][dr]],
                                bias=smalls[:, bcol + dr * 4 + ct:bcol + dr * 4 + ct + 1])
                for ct in cts:
                    st_ = ct % 2
                    xcc = xc[:, ct * W:(ct + 1) * W]
                    for dr in range(2):
                        R, I, Q, H = Rb[st_][dr][:, 0:W], Ib[st_][dr][:, 0:W], Qb[st_][dr][:, 0:W], Hb[st_][dr][:, 0:W]
                        ACT(R, R, AF.Exp, [B_R[st_][dr], B_small], [B_R[st_][dr]], scale=cch[:, dr * 4 + ct:dr * 4 + ct + 1])
                        TT(Q, R, R, ALU.mult, [B_R[st_][dr]], [B_Q[st_][dr]], eng="gpsimd")
                        TT(I, I, xcc, ALU.mult, [B_I[st_][dr], B_xcC[ct]], [B_I[st_][dr]], eng="gpsimd")
                    for dr in range(2):
                        Q = Qb[st_][dr][:, 0:W]
                        ACT(Q, Q, AF.Ln, [B_Q[st_][dr]], [B_Q[st_][dr]], scale=-1.0, bias=1.0)
                    for dr in range(2):
                        Q = Qb[st_][dr][:, 0:W]
                        ACT(Q, Q, AF.Exp, [B_Q[st_][dr]], [B_Q[st_][dr]], scale=0.5)
                    for dr in range(2):
                        R, I, Q, H = Rb[st_][dr][:, 0:W], Ib[st_][dr][:, 0:W], Qb[st_][dr][:, 0:W], Hb[st_][dr][:, 0:W]
                        TT(I, I, Q, ALU.mult, [B_I[st_][dr], B_Q[st_][dr]], [B_I[st_][dr]], eng="gpsimd")
                        if dr == 0:
                            P.op("vector", lambda e, H=H, R=R, I=I: e.tensor_tensor_scan(
                                out=H, data0=R, data1=I, initial=zcol, op0=ALU.mult, op1=ALU.add),
                                [B_R[st_][dr], B_I[st_][dr], B_small], [B_H[st_][dr]])
                            CP(lru_s0[:, 0, ct:ct + 1], H[:, W - 1:W], [B_H[st_][dr]], [B_s0])
                        else:
                            P.op("vector", lambda e, H=H, R=R, I=I: e.tensor_tensor_scan(
                                out=H[:, ::-1], data0=R[:, ::-1], data1=I[:, ::-1], initial=zcol, op0=ALU.mult, op1=ALU.add),
                                [B_R[st_][dr], B_I[st_][dr], B_small], [B_H[st_][dr]])
                            CP(lru_s0[:, 1, ct:ct + 1], H[:, 0:1], [B_H[st_][dr]], [B_s0])

        lru_ctx()
        lru_stage(T, hT, lambda s: B_hT[s * 4:(s + 1) * 4], False, extra_w=B_xcC + B_xcbC)

        if debug is not None and debug[0] == "lru":
            Bd = Buf()
            DMA("gpsimd", dbg_d[:, 0:4 * T], lruT.rearrange("p a b -> p (a b)"), [b for l in B_lruT for b in l], [Bd], "dbg")
            DMA("sync", dbg_d[:, 4 * T:4 * T + 8], lru_s0.rearrange("p a b -> p (a b)"), [B_s0], [Bd], "dbg")
            P.op("sync", lambda e: e.nop(), [Bd], ())
            P.finalize(block)
            return nc, A

        P.barrier()
        A.top = mark_persist
        Sb = A.alloc([NCH, 512], BF16)
        B_Sbn = [Buf() for _ in range(NCH)]
        wq = A.alloc([8, 2048], BF16)
        B_wq_kv, B_wq_qg = Buf(), Buf()
        B_wq = [B_wq_kv] * 8
        DMA("sync", wq[:, :, 512:1536], wqs[:, :, 512:1536], [B_wqs], [B_wq_kv], "c_wqkv")
        DMA("sync", wq[:, :, 0:512], wqs[:, :, 0:512], [B_wqs], [B_wq_qg], "c_wqqg")
        DMA("sync", wq[:, :, 1536:2048], wqs[:, :, 1536:2048], [B_wqs], [B_wq_qg], "c_wqqg")
        build_weight_conversion()
        rot = [A.alloc([128], F32) for _ in range(3)]
        B_rot = [Buf() for _ in range(3)]
        t1 = [A.alloc([512], F32) for _ in range(2)]
        t2 = [A.alloc([512], F32) for _ in range(2)]
        B_t1, B_t2 = [Buf(), Buf()], [Buf(), Buf()]
        qr = [A.alloc([512], BF16) for _ in range(2)]
        kr = [A.alloc([512], BF16) for _ in range(2)]
        vv = [A.alloc([512], BF16) for _ in range(2)]
        vw = [A.alloc([512], BF16) for _ in range(2)]
        sgt = [A.alloc([512], F32) for _ in range(2)]
        kqT = A.alloc([4, 4, 128], BF16)
        sT = A.alloc([4, 128], BF16)
        on = A.alloc([512], F32)
        retb = [A.alloc([512], BF16) for _ in range(2)]
        SfB = [A.alloc([512], BF16) for _ in range(2)]
        t1x = A.alloc([512], F32)
        SfRun2 = t1x
        SbRun2 = t1x
        B_Sf2 = Buf()
        B_Sb2 = B_Sf2
        bnst = A.alloc([4, 6], F32)
        mv = A.alloc([4, 2], F32)
        rs4 = A.alloc([2, 4], F32)
        mhalf = A.alloc([4], F32)
        nb4 = A.alloc([4], F32)
        B_qr, B_kr, B_vv, B_vw, B_sgt, B_retb = ([Buf(), Buf()] for _ in range(6))
        B_kqT, B_sT, B_on, B_bn, B_mh = (Buf() for _ in range(5))
        B_SfB = [Buf(), Buf()]
        MEMSET(mhalf, -0.5, [B_mh], eng="gpsimd")
        rot_ctr = [0]

        def rotary(src_bank, Bsrc, rt, Brt, dst, Bdst, i, add_eng="gpsimd"):
            p4 = bank(src_bank).rearrange("p (h a b) -> p h a b", h=4, a=2)
            cosb = rt[:, 0:64].unsqueeze(1).unsqueeze(1).broadcast_to([128, 4, 2, 64])
            sinb = rt[:, 64:128].unsqueeze(1).broadcast_to([128, 4, 64])
            a1 = t1[i].rearrange("p (h a b) -> p h a b", h=4, a=2)
            a2 = t2[i].rearrange("p (h a b) -> p h a b", h=4, a=2)
            TT(a1, p4, cosb, ALU.mult, [Bsrc, Brt], [B_t1[i]])
            STT(a2[:, :, 0, :], p4[:, :, 1, :], -1.0, sinb, ALU.mult, ALU.mult, [Bsrc, Brt], [B_t2[i]])
            TT(a2[:, :, 1, :], p4[:, :, 0, :], sinb, ALU.mult, [Bsrc, Brt], [B_t2[i]])
            TT(dst, t1[i], t2[i], ALU.add, [B_t1[i], B_t2[i]], [Bdst], eng=add_eng)

        def load_rot(n):
            s_ = rot_ctr[0] % 3
            rot_ctr[0] += 1
            DMA("sync", rot[s_], rot_d[n], (), [B_rot[s_]], "rot%d" % s_)
            return s_

        def passA_front(n):
            i = n % 2
            rsl = load_rot(n)
            tk = slice(n * 128, (n + 1) * 128)
            for kc in range(8):
                MM(bank(i), hT[:, kc, tk], wq[:, kc, 512:1024], kc == 0, kc == 7, [B_hT[n], B_wq[kc]], [Bps[i]])
                MM(bank(2 + i), hT[:, kc, tk], wq[:, kc, 1024:1536], kc == 0, kc == 7, [B_hT[n], B_wq[kc]], [Bps[2 + i]])
            rotary(i, Bps[i], rot[rsl], B_rot[rsl], kr[i], B_kr[i], i)
            for h in range(4):
                ACT(vw[i][:, h * 128:(h + 1) * 128], bank(2 + i)[:, h * 128:(h + 1) * 128], AF.Identity, [Bps[2 + i], B_small], [B_vw[i]],
                    scale=posw[:, 1, h:h + 1])

        def passA_back(n):
            i = n % 2
            ub = 4 + i
            for h in range(4):
                MM(bank(ub)[:, h * 128:(h + 1) * 128], kr[i][:, h * 128:(h + 1) * 128], vw[i][:, h * 128:(h + 1) * 128], True, True,
                   [B_kr[i], B_vw[i]], [Bps[ub]])
            src, dst = (SbRun, SbRun2) if n % 2 == 1 else (SbRun2, SbRun)
            Bs_, Bd_ = (B_Sb, B_Sb2) if n % 2 == 1 else (B_Sb2, B_Sb)
            ACT(Sb[:, n, :], src, AF.Identity, [Bs_], [B_Sbn[n]])
            for h in range(4):
                STT(dst[:, h * 128:(h + 1) * 128], src[:, h * 128:(h + 1) * 128], Gd[:, 4 + h:5 + h],
                    bank(ub)[:, h * 128:(h + 1) * 128], ALU.mult, ALU.add, [Bs_, Bps[ub], B_small], [Bd_])

        passA_front(NCH - 1)
        for n in range(NCH - 1, -1, -1):
            if n > 0:
                passA_front(n - 1)
            passA_back(n)
            emit_conv_piece()

        def passB_front_mm(n):
            tk = slice(n * 128, (n + 1) * 128)
            for kc in range(8):
                for j in range(4):
                    MM(bank(j), hT[:, kc, tk], wq[:, kc, j * 512:(j + 1) * 512], kc == 0, kc == 7, [B_hT[n], B_wq_kv, B_wq_qg], [Bps[j]])

        def passB_front_ew(n):
            i = n % 2
            rsl = load_rot(n)
            rotary(0, Bps[0], rot[rsl], B_rot[rsl], qr[i], B_qr[i], 0)
            rotary(1, Bps[1], rot[rsl], B_rot[rsl], kr[i], B_kr[i], 1)
            ACT(vv[i], bank(2), AF.Identity, [Bps[2]], [B_vv[i]])
            for h in range(4):
                ACT(vw[i][:, h * 128:(h + 1) * 128], bank(2)[:, h * 128:(h + 1) * 128], AF.Identity, [Bps[2], B_small], [B_vw[i]],
                    scale=posw[:, 0, h:h + 1])
            ACT(sgt[i], bank(3), AF.Silu, [Bps[3]], [B_sgt[i]])

        def passB_mid1a(n):
            i = n % 2
            pb = bankb(4).rearrange("p (a b) -> p a b", a=8)
            for h in range(4):
                TR(pb[:, h, :], kr[i][:, h * 128:(h + 1) * 128], identB, [B_kr[i], B_small], [Bps[4]])
            for h in range(4):
                TR(pb[:, 4 + h, :], qr[i][:, h * 128:(h + 1) * 128], identB, [B_qr[i], B_small], [Bps[4]])
            ACT(kqT[:, 0:2, :, :].rearrange("p a b c -> p (a b c)"), bankb(4), AF.Identity, [Bps[4]], [B_kqT])
            TT(kqT[:, 2, :, :], pb[:, 4:8, :], WQT[:, 0, :, :], ALU.mult, [Bps[4], B_small], [B_kqT])
            TT(kqT[:, 3, :, :], pb[:, 4:8, :], WQT[:, 1, :, :], ALU.mult, [Bps[4], B_small], [B_kqT])
            srcf = SfRun if n % 2 == 0 else SfRun2
            Bsf = B_Sf if n % 2 == 0 else B_Sf2
            ACT(SfB[n % 2], srcf, AF.Identity, [Bsf], [B_SfB[n % 2]])

        def passB_mid1b(n):
            i = n % 2
            for h in range(4):
                hs = slice(h * 128, (h + 1) * 128)
                MM(bank(7)[:, hs], kr[i][:, hs], vw[i][:, hs], True, True, [B_kr[i], B_vw[i]], [Bps[7]])

        def passB_tail(n):
            i = n % 2
            tk = slice(n * 128, (n + 1) * 128)
            pb2 = bankb(6).rearrange("p (a b) -> p a b", a=8)
            for h in range(4):
                TR(pb2[:, h, :], retb[i][:, h * 128:(h + 1) * 128], identB, [B_retb[i], B_small], [Bps[6]])
            ACT(hT[:, 0:4, tk], pb2[:, 0:4, :], AF.Identity, [Bps[6]], [B_hT[n], B_ret[n]])

        def passB_mid2a(n):
            for h in range(4):
                MM(bank(5)[:, h * 128:(h + 1) * 128], kqT[:, 0, h, :], kqT[:, 1, h, :], True, True, [B_kqT], [Bps[5]])
            TT(sT.rearrange("p a b -> p (a b)"), bank(5), Dm.rearrange("p a b -> p (a b)"), ALU.mult, [Bps[5], B_small], [B_sT])

        def passB_mid2b(n):
            i = n % 2
            for h in range(4):
                hs = slice(h * 128, (h + 1) * 128)
                MM(bank(5)[:, hs], sT[:, h, :], vv[i][:, hs], True, False, [B_sT, B_vv[i]], [Bps[5]])
                MM(bank(5)[:, hs], kqT[:, 2, h, :], SfB[n % 2][:, hs], False, False, [B_kqT, B_SfB[n % 2]], [Bps[5]])
                MM(bank(5)[:, hs], kqT[:, 3, h, :], Sb[:, n, hs], False, True, [B_kqT, B_Sbn[n]], [Bps[5]])
            for h in range(4):
                hs = slice(h * 128, (h + 1) * 128)
                srcf, dstf = (SfRun, SfRun2) if n % 2 == 0 else (SfRun2, SfRun)
                Bsf, Bdf = (B_Sf, B_Sf2) if n % 2 == 0 else (B_Sf2, B_Sf)
                STT(dstf[:, hs], srcf[:, hs], Gd[:, h:h + 1], bank(7)[:, hs], ALU.mult, ALU.add, [Bsf, Bps[7], B_small], [Bdf])
            for h in range(4):
                hs = slice(h * 128, (h + 1) * 128)
                P.op("vector", lambda e, h=h, hs=hs: e.bn_stats(out=bnst[:, h, :], in_=bank(5)[:, hs]), [Bps[5]], [B_bn])
            for h in range(4):
                P.op("vector", lambda e, h=h: e.bn_aggr(out=mv[:, h, :], in_=bnst[:, h, :]), [B_bn], [B_bn])
            ACT(rs4[:, 0, :], mv[:, :, 1], AF.Sqrt, [B_bn], [B_bn], bias=EPS)
            P.op("vector", lambda e: e.reciprocal(out=rs4[:, 1, :], in_=rs4[:, 0, :]), [B_bn], [B_bn])
            STT(nb4, mv[:, :, 0], -1.0, rs4[:, 1, :], ALU.mult, ALU.mult, [B_bn], [B_bn])
            for h in range(4):
                hs = slice(h * 128, (h + 1) * 128)
                ACT(on[:, hs], bank(5)[:, hs], AF.Identity, [Bps[5], B_bn], [B_on], scale=rs4[:, 1, h:h + 1], bias=nb4[:, h:h + 1])
            TT(retb[i], on, sgt[i], ALU.mult, [B_on, B_sgt[i]], [B_retb[i]], eng="gpsimd")

        passB_front_mm(0)
        passB_front_ew(0)
        for n in range(NCH):
            passB_mid1a(n)
            if n + 1 < NCH:
                passB_front_mm(n + 1)
            passB_mid2a(n)
            if n + 1 < NCH:
                passB_front_ew(n + 1)
            passB_mid1b(n)
            if n > 0:
                passB_tail(n - 1)
            passB_mid2b(n)
            emit_conv_piece()
        passB_tail(NCH - 1)
        while conv_pieces:
            emit_conv_piece()
        for bl, key in ((B_w1s, "cv_w1"), (B_w2s, "cv_w2"), (B_wos, "cv_wo")):
            for b_ in bl:
                b_.w = conv_last[key]

        if debug is not None and debug[0] == "ret":
            Bd = Buf()
            DMA("gpsimd", dbg_d, hT[:, 0:4, :].rearrange("p a b -> p (a b)"), B_ret + B_hT, [Bd], "dbg")
            P.op("sync", lambda e: e.nop(), [Bd], ())
            P.finalize(block)
            return nc, A

        P.barrier()
        A.top = mark_persist
        g1b = A.alloc([D], F32)
        g2b = A.alloc([D], F32)
        fgb = A.alloc([D], F32)
        B_g = Buf()
        DMA("sync", g1b, gsc[0].partition_broadcast(128), [B_gsc], [B_g], "c_g")
        DMA("sync", g2b, gsc[1].partition_broadcast(128), [B_gsc], [B_g], "c_g")
        DMA("sync", fgb, fg_d, (), [B_g], "c_g")
        NWS = 3
        wst = [A.alloc([4096], BF16) for _ in range(NWS)]
        B_wst = [Buf() for _ in range(NWS)]
        xs = [A.alloc([D], F32) for _ in range(8)]
        B_xs = [Buf() for _ in range(8)]
        h2 = [A.alloc([D], BF16) for _ in range(4)]
        B_h2 = [Buf() for _ in range(4)]
        h2T = [A.alloc([8, 512], BF16) for _ in range(2)]
        B_h2T = [Buf(), Buf()]
        hidT = hT[:, 4:8, :].rearrange("p a (b c) -> p (a b) c", c=512)
        B_hid = [Buf() for _ in range(32)]
        rl = [A.alloc([512], F32) for _ in range(2)]
        B_rl = [Buf(), Buf()]
        ytmp = rl
        B_yt = B_rl
        ws_ctr = [0]

        def wload(src, Bsrc):
            s_ = ws_ctr[0] % NWS
            ws_ctr[0] += 1
            DMA("sync", wst[s_], src, Bsrc, [B_wst[s_]], "ws%d" % s_)
            return s_

        def mixT(kc):
            return hT[:, kc, :] if kc < 4 else lruT[:, kc - 4, :]

        def op_a(grp):
            g0 = grp * 512
            wo = [wload(wos[u], [B_wos[u]]) for u in range(2)]
            for tt in range(4):
                n = grp * 4 + tt
                xi = (grp % 2) * 4 + tt
                tk = slice(g0 + tt * 128, g0 + (tt + 1) * 128)
                DMA("gpsimd", xs[xi], x_d[tk, :], (), [B_xs[xi]], "x3_%d" % xi)
                for kc in range(8):
                    u = kc // 4
                    wv = wst[wo[u]].rearrange("p (kc n) -> p kc n", kc=4)
                    rd = [B_ret[n], B_hT[n]] if kc < 4 else [B_lruT[kc - 4][grp]]
                    for fh in range(2):
                        MM(bank(4 + fh), mixT(kc)[:, tk], wv[:, kc % 4, fh * 512:(fh + 1) * 512], kc == 0, kc == 7,
                           rd + [B_wst[wo[u]]], [Bps[4 + fh]])
                for fh in range(2):
                    fs = slice(fh * 512, (fh + 1) * 512)
                    TT(ytmp[fh], bank(4 + fh), g1b[:, fs], ALU.mult, [Bps[4 + fh], B_g], [B_yt[fh]])
                    TT(xs[xi][:, fs], ytmp[fh], xs[xi][:, fs], ALU.add, [B_yt[fh], B_xs[xi]], [B_xs[xi]])
                norm_pre(xs[xi], B_xs[xi], h2[tt], B_h2[tt])

        def op_b_tile(grp, tt):
            hb_ = grp % 2
            norm_post(h2[tt], B_h2[tt], 1, 2, lambda kc, tt=tt: h2T[hb_][:, kc, tt * 128:(tt + 1) * 128], B_h2T[hb_], 6)

        def op_b(grp):
            for tt in range(4):
                op_b_tile(grp, tt)

        def mlp1(grp):
            hb2 = grp % 2
            for u in range(8):
                s_ = wload(w1s[u], [B_w1s[u]])
                wv = wst[s_].rearrange("p (kc j) -> p kc j", kc=8)
                for hb_ in range(4):
                    H = u * 4 + hb_
                    bk = (4, 5, 7)[H % 3]
                    for kc in range(8):
                        MM(bank(bk), wv[:, kc, hb_ * 128:(hb_ + 1) * 128], h2T[hb2][:, kc, :], kc == 0, kc == 7,
                           [B_wst[s_], B_h2T[hb2]], [Bps[bk]])
                    ACT(rl[H % 2], bank(bk), AF.Relu, [Bps[bk]], [B_rl[H % 2]])
                    STT(hidT[:, H, :], bank(bk), 0.0, rl[H % 2], ALU.max, ALU.mult, [Bps[bk], B_rl[H % 2]], [B_hid[H]])

        def mlp2(grp, fh, hook=None):
            fs = slice(fh * 512, (fh + 1) * 512)
            for u4 in range(4):
                if hook is not None:
                    hook(u4)
                s_ = wload(w2s[fh * 4 + u4], [B_w2s[fh * 4 + u4]])
                wv = wst[s_].rearrange("p (hb j) -> p hb j", hb=8)
                for tt in range(4):
                    for hb_ in range(8):
                        H = u4 * 8 + hb_
                        MM(bank(tt), hidT[:, H, tt * 128:(tt + 1) * 128], wv[:, hb_, :], H == 0, H == 31,
                           [B_hid[H], B_wst[s_]], [Bps[tt]])
            for tt in range(4):
                xi = (grp % 2) * 4 + tt
                TT(ytmp[tt % 2], bank(tt), g2b[:, fs], ALU.mult, [Bps[tt], B_g], [B_yt[tt % 2]])
                TT(xs[xi][:, fs], ytmp[tt % 2], xs[xi][:, fs], ALU.add, [B_yt[tt % 2], B_xs[xi]], [B_xs[xi]])

        def fin(grp):
            g0 = grp * 512
            for tt in range(4):
                xi = (grp % 2) * 4 + tt
                tk = slice(g0 + tt * 128, g0 + (tt + 1) * 128)
                s_ = stat_ctr[0] % 8
                stat_ctr[0] += 1
                st, Bst = stats[:, s_, :], B_stats[s_]
                ACT(h2[tt], xs[xi], AF.Square, [B_xs[xi]], [B_h2[tt], Bst], accum=st[:, 0:1])
                ACT(st[:, 1:2], st[:, 0:1], AF.Sqrt, [Bst], [Bst], scale=1.0 / D, bias=EPS)
                P.op("vector", lambda e, st=st: e.reciprocal(out=st[:, 2:3], in_=st[:, 1:2]), [Bst], [Bst])
                STT(xs[xi], xs[xi], st[:, 2:3], fgb, ALU.mult, ALU.mult, [B_xs[xi], Bst, B_g], [B_xs[xi]])
                DMA("gpsimd", out_d[tk, :], xs[xi], [B_xs[xi]], [B_xs[xi]], "o3_%d" % xi)

        op_a(0)
        op_b(0)
        for grp in range(8):
            mlp1(grp)
            if grp + 1 < 8:
                op_a(grp + 1)
            mlp2(grp, 0)
            if grp + 1 < 8:
                mlp2(grp, 1, hook=lambda u4, grp=grp: op_b_tile(grp + 1, u4))
            else:
                mlp2(grp, 1)
            fin(grp)
        P.op("sync", lambda e: e.nop(), B_xs, ())
        P.finalize(block)
    return nc, A


def _fm(v):
    return np.ascontiguousarray(np.asarray(v, np.float32).reshape(-1, 128).T)


def host_consts():
    c = np.arange(128, dtype=np.float32)
    posc = np.stack([127 - c, c, c + 1, 128 - c, 255 - c, 127 - c, c, 128 + c], axis=1).astype(np.float32)
    m = c[:, None]
    cc = c[None, :]
    rel = cc - m
    mk = np.stack([np.maximum(rel, 0), (rel >= 0).astype(np.float32), np.maximum(-rel, 0), (rel <= 0).astype(np.float32),
                   np.broadcast_to(cc + 1, (128, 128)), np.broadcast_to(128 - cc, (128, 128))], axis=1).astype(np.float32)
    t = np.arange(T)
    row = (t // 64).astype(np.float32)
    col = (t % 64).astype(np.float32)
    nf = 32
    inv = (np.float32(10000.0) ** (-np.arange(nf, dtype=np.float32) / np.float32(nf))).astype(np.float32)
    ang = np.concatenate([row[:, None] * inv, col[:, None] * inv], axis=-1).astype(np.float32)
    rot = np.concatenate([np.cos(ang), np.sin(ang)], axis=-1).astype(np.float32).reshape(NCH, 128, 128)
    return posc, np.ascontiguousarray(mk.reshape(128, 6 * 128)), np.ascontiguousarray(rot)


def make_in_maps(inputs, cores):
    f = lambda k: np.asarray(inputs[k], np.float32)
    posc, mk, rot = host_consts()
    conv_w = f("conv_w")[0]
    cw = np.ascontiguousarray(conv_w.reshape(4, 4, 128).transpose(2, 1, 0).reshape(128, 16))
    cb = _fm(f("conv_b")[0])
    d4 = lambda a: np.ascontiguousarray(a.reshape(2, 4, 128).transpose(2, 0, 1).reshape(128, 8))

    def bd(w):
        o = np.zeros((128, 2, 4, 128), np.float32)
        for dr in range(2):
            for ct in range(4):
                o[0:64, dr, ct, 0:64] = w[dr, 2 * ct]
                o[64:128, dr, ct, 64:128] = w[dr, 2 * ct + 1]
        return np.ascontiguousarray(o.reshape(128, 1024))

    shared = dict(
        w_ada=np.ascontiguousarray(f("w_ada")[0]), b_ada=np.ascontiguousarray(f("b_ada")[0].reshape(1, -1)),
        w_in=np.ascontiguousarray(f("w_in")[0]), w_out=np.ascontiguousarray(f("w_out")[0]),
        w_mlp1=np.ascontiguousarray(f("w_mlp1")[0]), w_mlp2=np.ascontiguousarray(f("w_mlp2")[0]),
        final_g_rep=np.ascontiguousarray(np.broadcast_to(f("final_g")[None, :], (128, D))),
        wabd=bd(f("lru_wa")[0]), wxbd=bd(f("lru_wx")[0]),
        ident=np.eye(128, dtype=np.float32), mk=mk, rot=rot)
    rdrep = np.ascontiguousarray(np.broadcast_to(f("ret_decay")[0].reshape(1, 8), (128, 8)))
    maps = []
    for b in cores:
        smalls = np.concatenate([
            rdrep, _fm(f("norm1_g")[0]), _fm(f("norm2_g")[0]), _fm(f("c")[b]), _fm(f("c_ctx")),
            cw, cb, d4(f("lru_ba")[0]), d4(f("lru_bx")[0]), d4(f("lru_lambda")[0]), posc], axis=1).astype(np.float32)
        assert smalls.shape == (128, NS)
        m = dict(shared)
        m["x"] = np.ascontiguousarray(f("x")[b])
        m["ctx"] = np.ascontiguousarray(f("ctx")[b])
        m["smalls"] = np.ascontiguousarray(smalls)
        maps.append(m)
    return maps


def kernel(**inputs):
    nc, _ = build_program()
    maps = make_in_maps(inputs, list(range(8)))
    res = run_bass_kernel_spmd(nc, maps, core_ids=list(range(8)))
    out = np.stack([np.asarray(r["out"], np.float32) for r in res.results], axis=0)
    return out
```
